# Optimizing a Trainium2 kernel written in Bass

```python
import jax, jax.numpy as jnp
from jax import lax
import numpy as np

D_MODEL = 1024
BATCH = 8
SEQ = 2048
DEPTH = 2

GRID_W = 64
CTX_LEN = 256
HEAD_DIM = 64
RET_HEADS = 4
RET_DK = 64
RET_DV = 128
RET_CHUNK = 128
NA_HEADS = 8
NA_ROWS = 8
NA_COLS = 16
GQA_Q_HEADS = 8
GQA_KV_HEADS = 2
GQA_GROUP = GQA_Q_HEADS // GQA_KV_HEADS
Q_BLOCK = 128
D_FF = 4 * D_MODEL
ROPE_BASE = 10000.0
NORM_EPS = 1e-6
NEG_INF = -1e30

RET_QK_W = RET_HEADS * RET_DK
RET_V_W = RET_HEADS * RET_DV
NA_W = NA_HEADS * HEAD_DIM
GQA_Q_W = GQA_Q_HEADS * HEAD_DIM
GQA_KV_W = GQA_KV_HEADS * HEAD_DIM
SPLIT_SIZES = (RET_QK_W, RET_QK_W, RET_V_W, RET_V_W, NA_W, NA_W, NA_W,
               GQA_Q_W, GQA_KV_W, GQA_KV_W, D_MODEL, D_MODEL, D_MODEL)
SPLIT_POINTS = tuple(int(v) for v in np.cumsum(SPLIT_SIZES)[:-1])
IN_W = int(sum(SPLIT_SIZES))

kernel_name = "hybrid_retention_natten_gqa_dit_block"


def rms_norm(x, gain=None):
    xf = x.astype(jnp.float32)
    y = xf * lax.rsqrt(jnp.mean(xf * xf, axis=-1, keepdims=True) + NORM_EPS)
    if gain is not None:
        y = y * gain.astype(jnp.float32)
    return y.astype(x.dtype)


def heads(t, n_heads):
    return t.reshape(t.shape[:-1] + (n_heads, t.shape[-1] // n_heads))


def axial_rope_angles(n):
    t = jnp.arange(n, dtype=jnp.int32)
    pos = jnp.stack([t // GRID_W, t % GRID_W], axis=-1).astype(jnp.float32)
    n_freq = HEAD_DIM // 4
    inv_freq = ROPE_BASE ** (-jnp.arange(n_freq, dtype=jnp.float32) / n_freq)
    ang = pos[:, :, None] * inv_freq
    return jnp.cos(ang), jnp.sin(ang)


def apply_rope(x, cos, sin):
    b, n, h, d = x.shape
    xf = x.astype(jnp.float32).reshape(b, n, h, 2, 2, d // 4)
    x1, x2 = xf[..., 0, :], xf[..., 1, :]
    cs = cos[None, :, None]
    sn = sin[None, :, None]
    out = jnp.stack([x1 * cs - x2 * sn, x2 * cs + x1 * sn], axis=-2)
    return out.reshape(b, n, h, d).astype(x.dtype)


def attend(q, k, v):
    s = jnp.einsum('bqkgd,bskd->bkgqs', q, k, preferred_element_type=jnp.float32) * (q.shape[-1] ** -0.5)
    p = jax.nn.softmax(s, axis=-1).astype(v.dtype)
    return jnp.einsum('bkgqs,bskd->bqkgd', p, v)


def retention_scan(q, k, v, log_gamma, s0, include_diag, with_output):
    b, L, h, _ = q.shape
    dv = v.shape[-1]
    nc = L // RET_CHUNK
    pos = jnp.arange(RET_CHUNK, dtype=jnp.float32)
    rel = pos[:, None] - pos[None, :]
    keep = (rel >= 0) if include_diag else (rel > 0)
    lg = log_gamma.astype(jnp.float32)
    decay_in = jnp.where(keep[None], jnp.exp(lg[:, None, None] * jnp.maximum(rel, 0.0)[None]), 0.0)
    decay_q = jnp.exp(lg[None, :] * (pos[:, None] + 1.0))
    decay_k = jnp.exp(lg[None, :] * (RET_CHUNK - 1.0 - pos[:, None]))
    decay_s = jnp.exp(lg * RET_CHUNK)

    def chunks(t):
        return t.astype(jnp.float32).reshape(b, nc, RET_CHUNK, h, t.shape[-1]).swapaxes(0, 1)

    def step(s, blk):
        qb, kb, vb = blk
        s_new = s * decay_s[None, :, None, None] + jnp.einsum('bjhd,bjhe->bhde', kb * decay_k[None, :, :, None], vb)
        if not with_output:
            return s_new, None
        a = jnp.einsum('bihd,bjhd->bhij', qb, kb) * decay_in[None]
        o = (jnp.einsum('bhij,bjhe->bihe', a, vb)
             + jnp.einsum('bihd,bhde->bihe', qb, s) * decay_q[None, :, :, None])
        return s_new, o

    s_fin, o = lax.scan(step, s0, (chunks(q), chunks(k), chunks(v)))
    if not with_output:
        return None, s_fin
    return o.swapaxes(0, 1).reshape(b, L, h, dv), s_fin


def retention_bidir(q, k, v, log_gamma, s0_fwd, s0_bwd, with_output):
    o_f, s_f = retention_scan(q, k, v, log_gamma[0], s0_fwd, True, with_output)
    o_b, s_b = retention_scan(q[:, ::-1], k[:, ::-1], v[:, ::-1], log_gamma[1], s0_bwd, False, with_output)
    if not with_output:
        return None, s_f, s_b
    return o_f + o_b[:, ::-1], s_f, s_b


def retention_out(o, gate):
    y = rms_norm(o).reshape(o.shape[:2] + (RET_V_W,))
    return (y * jax.nn.silu(gate.astype(jnp.float32))).astype(gate.dtype)


def neighbourhood_attention(q, k, v, k_ctx, v_ctx, rel_bias):
    b, n, h, d = q.shape
    rows = n // GRID_W
    kr = min(NA_ROWS, rows)
    q = q.reshape(b, rows, GRID_W, h, d)
    k = k.reshape(b, rows, GRID_W, h, d)
    v = v.reshape(b, rows, GRID_W, h, d)
    col = jnp.arange(GRID_W)
    col_start = jnp.clip(col - NA_COLS // 2, 0, GRID_W - NA_COLS)
    in_window = (col[None, :] >= col_start[:, None]) & (col[None, :] < col_start[:, None] + NA_COLS)
    dcol = jnp.clip(col[None, :] - col[:, None], 1 - NA_COLS, NA_COLS - 1) + NA_COLS - 1
    col_bias = jnp.where(in_window[None, None], rel_bias[:, :, dcol].astype(jnp.float32), NEG_INF)
    scale = d ** -0.5

    def row_block(r):
        r0 = jnp.clip(r - kr // 2, 0, rows - kr)
        qr = lax.dynamic_index_in_dim(q, r, axis=1, keepdims=False)
        kb = lax.dynamic_slice_in_dim(k, r0, kr, axis=1)
        vb = lax.dynamic_slice_in_dim(v, r0, kr, axis=1)
        drow = r0 + jnp.arange(kr) - r + NA_ROWS - 1
        bias = jnp.take(col_bias, drow, axis=1).transpose(0, 2, 1, 3)
        s_loc = jnp.einsum('bqhd,bikhd->bhqik', qr, kb, preferred_element_type=jnp.float32) * scale + bias[None]
        s_ctx = jnp.einsum('bqhd,bmhd->bhqm', qr, k_ctx, preferred_element_type=jnp.float32) * scale
        s = jnp.concatenate([s_loc.reshape(b, h, GRID_W, kr * GRID_W), s_ctx], axis=-1)
        p = jax.nn.softmax(s, axis=-1).astype(v.dtype)
        p_loc = p[..., :kr * GRID_W].reshape(b, h, GRID_W, kr, GRID_W)
        p_ctx = p[..., kr * GRID_W:]
        return (jnp.einsum('bhqik,bikhd->bqhd', p_loc, vb)
                + jnp.einsum('bhqm,bmhd->bqhd', p_ctx, v_ctx))

    out = lax.map(row_block, jnp.arange(rows))
    return out.swapaxes(0, 1).reshape(b, n, h * d)


def gqa_latent(q, k, v, k_ctx, v_ctx):
    b, n, _, d = q.shape
    keys = jnp.concatenate([k_ctx, k], axis=1)
    vals = jnp.concatenate([v_ctx, v], axis=1)
    qb = q.reshape(b, n // Q_BLOCK, Q_BLOCK, GQA_KV_HEADS, GQA_GROUP, d).swapaxes(0, 1)
    out = lax.map(lambda blk: attend(blk, keys, vals), qb)
    return out.swapaxes(0, 1).reshape(b, n, GQA_Q_W)


def merge_branches(y_ret, y_na, y_gqa, ga, gb, gc, w_br_ret, w_br_na, w_br_gqa, w_out):
    y = (jax.nn.sigmoid(ga) * (y_ret @ w_br_ret)
         + jax.nn.sigmoid(gb) * (y_na @ w_br_na)
         + jax.nn.sigmoid(gc) * (y_gqa @ w_br_gqa))
    return y @ w_out


def token_mixer(h, hc, w_in, log_gamma, rel_bias, q_gain, k_gain,
                w_br_ret, w_br_na, w_br_gqa, w_out, cos, sin, with_ctx_out):
    b, _, _ = h.shape
    m = hc.shape[1]
    (rq, rk, rv, rg, nq, nk, nv, gq, gk, gv, ga, gb, gc) = jnp.split(h @ w_in, SPLIT_POINTS, axis=-1)
    (crq, crk, crv, crg, cnq, cnk, cnv, cgq, cgk, cgv, cga, cgb, cgc) = jnp.split(hc @ w_in, SPLIT_POINTS, axis=-1)

    ret_scale = RET_DK ** -0.5
    s0 = jnp.zeros((b, RET_HEADS, RET_DK, RET_DV), jnp.float32)
    o_ctx_ret, s_fwd, s_bwd = retention_bidir(
        heads(crq, RET_HEADS), heads(crk, RET_HEADS) * ret_scale, heads(crv, RET_HEADS),
        log_gamma, s0, s0, with_ctx_out)
    o_ret, _, _ = retention_bidir(
        apply_rope(heads(rq, RET_HEADS), cos, sin),
        apply_rope(heads(rk, RET_HEADS), cos, sin) * ret_scale,
        heads(rv, RET_HEADS), log_gamma, s_fwd, s_bwd, True)
    y_ret = retention_out(o_ret, rg)

    cnk_h, cnv_h = heads(cnk, NA_HEADS), heads(cnv, NA_HEADS)
    y_na = neighbourhood_attention(heads(nq, NA_HEADS), heads(nk, NA_HEADS), heads(nv, NA_HEADS),
                                   cnk_h, cnv_h, rel_bias)

    cgk_h = rms_norm(heads(cgk, GQA_KV_HEADS), k_gain)
    cgv_h = heads(cgv, GQA_KV_HEADS)
    q_c = apply_rope(rms_norm(heads(gq, GQA_Q_HEADS), q_gain), cos, sin)
    k_c = apply_rope(rms_norm(heads(gk, GQA_KV_HEADS), k_gain), cos, sin)
    y_gqa = gqa_latent(q_c, k_c, heads(gv, GQA_KV_HEADS), cgk_h, cgv_h)

    y = merge_branches(y_ret, y_na, y_gqa, ga, gb, gc, w_br_ret, w_br_na, w_br_gqa, w_out)
    if not with_ctx_out:
        return y, None

    yc_ret = retention_out(o_ctx_ret, crg)
    yc_na = attend(heads(cnq, NA_HEADS)[:, :, :, None, :], cnk_h, cnv_h).reshape(b, m, NA_W)
    cq = rms_norm(heads(cgq, GQA_Q_HEADS), q_gain).reshape(b, m, GQA_KV_HEADS, GQA_GROUP, HEAD_DIM)
    yc_gqa = attend(cq, cgk_h, cgv_h).reshape(b, m, GQA_Q_W)
    yc = merge_branches(yc_ret, yc_na, yc_gqa, cga, cgb, cgc, w_br_ret, w_br_na, w_br_gqa, w_out)
    return y, yc


def squared_relu_mlp(h, w1, w2):
    return jnp.square(jax.nn.relu(h @ w1)) @ w2


def setup_inputs(seed: int = 0) -> dict:
    key = jax.random.key(seed)
    ks = jax.random.split(key, 24)
    f32 = jnp.float32
    d = D_MODEL

    def nrm(k, shape, scale):
        return jax.random.normal(k, shape, f32) * scale

    decay_base = jnp.log(2.0 ** (5.0 + jnp.arange(RET_HEADS, dtype=f32)) - 1.0)
    return {
        "x": nrm(ks[0], (BATCH, SEQ, d), 1.0),
        "c": nrm(ks[1], (BATCH, d), 1.0),
        "ctx": nrm(ks[2], (BATCH, CTX_LEN, d), 1.0),
        "c_ctx": nrm(ks[3], (d,), 1.0),
        "w_mod": nrm(ks[4], (DEPTH, d, 6 * d), 0.5 * d ** -0.5),
        "b_mod": nrm(ks[5], (DEPTH, 6 * d), 0.02),
        "g_pre_mix": 1.0 + nrm(ks[6], (DEPTH, d), 0.1),
        "g_post_mix": 1.0 + nrm(ks[7], (DEPTH, d), 0.1),
        "g_pre_mlp": 1.0 + nrm(ks[8], (DEPTH, d), 0.1),
        "g_post_mlp": 1.0 + nrm(ks[9], (DEPTH, d), 0.1),
        "w_in": nrm(ks[10], (DEPTH, d, IN_W), d ** -0.5),
        "ret_decay_logit": decay_base + nrm(ks[11], (DEPTH, 2, RET_HEADS), 0.1),
        "na_rel_bias": nrm(ks[12], (DEPTH, NA_HEADS, 2 * NA_ROWS - 1, 2 * NA_COLS - 1), 0.1),
        "gqa_q_norm": 1.0 + nrm(ks[13], (DEPTH, HEAD_DIM), 0.1),
        "gqa_k_norm": 1.0 + nrm(ks[14], (DEPTH, HEAD_DIM), 0.1),
        "w_br_ret": nrm(ks[15], (DEPTH, RET_V_W, d), RET_V_W ** -0.5),
        "w_br_na": nrm(ks[16], (DEPTH, NA_W, d), NA_W ** -0.5),
        "w_br_gqa": nrm(ks[17], (DEPTH, GQA_Q_W, d), GQA_Q_W ** -0.5),
        "w_out": nrm(ks[18], (DEPTH, d, d), d ** -0.5),
        "w_mlp_in": nrm(ks[19], (DEPTH, d, D_FF), d ** -0.5),
        "w_mlp_out": nrm(ks[20], (DEPTH, D_FF, d), D_FF ** -0.5),
    }


def reference(x, c, ctx, c_ctx, w_mod, b_mod, g_pre_mix, g_post_mix, g_pre_mlp, g_post_mlp,
              w_in, ret_decay_logit, na_rel_bias, gqa_q_norm, gqa_k_norm,
              w_br_ret, w_br_na, w_br_gqa, w_out, w_mlp_in, w_mlp_out):
    cos, sin = axial_rope_angles(x.shape[1])
    silu_c = jax.nn.silu(c)
    silu_cc = jax.nn.silu(c_ctx)
    for l in range(DEPTH):
        last = l == DEPTH - 1
        mod_lat = (silu_c @ w_mod[l] + b_mod[l])[:, None, :]
        mod_ctx = silu_cc @ w_mod[l] + b_mod[l]
        sh1, sc1, gt1, sh2, sc2, gt2 = jnp.split(mod_lat, 6, axis=-1)
        csh1, csc1, cgt1, csh2, csc2, cgt2 = jnp.split(mod_ctx, 6, axis=-1)
        log_gamma = jax.nn.log_sigmoid(ret_decay_logit[l].astype(jnp.float32))

        h = rms_norm(x, g_pre_mix[l]) * (1.0 + sc1) + sh1
        hc = rms_norm(ctx, g_pre_mix[l]) * (1.0 + csc1) + csh1
        y, yc = token_mixer(h, hc, w_in[l], log_gamma, na_rel_bias[l], gqa_q_norm[l], gqa_k_norm[l],
                            w_br_ret[l], w_br_na[l], w_br_gqa[l], w_out[l], cos, sin, not last)
        x = x + gt1 * rms_norm(y, g_post_mix[l])
        h = rms_norm(x, g_pre_mlp[l]) * (1.0 + sc2) + sh2
        x = x + gt2 * rms_norm(squared_relu_mlp(h, w_mlp_in[l], w_mlp_out[l]), g_post_mlp[l])

        if not last:
            ctx = ctx + cgt1 * rms_norm(yc, g_post_mix[l])
            hc = rms_norm(ctx, g_pre_mlp[l]) * (1.0 + csc2) + csh2
            ctx = ctx + cgt2 * rms_norm(squared_relu_mlp(hc, w_mlp_in[l], w_mlp_out[l]), g_post_mlp[l])
    return x
```

```python
import contextlib
import numpy as np
import concourse.bass as bass
import concourse.mybir as mybir
from concourse.bass_utils import run_bass_kernel_spmd

F32 = mybir.dt.float32
BF16 = mybir.dt.bfloat16
AF = mybir.ActivationFunctionType
ALU = mybir.AluOpType

COMPUTE = ("pe", "act", "dve", "pool")
SAME_ENG_SYNC = True
EPS = 1e-6
NEG = -1e30

RQD, RK, RV, RG, NQ, NK, NV, GQ, GK, GV, GA, GB, GC, W1C = (
    0, 512, 768, 1280, 1792, 2304, 2816, 3328, 3840, 3968, 4096, 5120, 6144, 7168)


class Sched:
    PH = 0

    def __init__(self, nc):
        self.nc = nc
        self.ops = []
        self.last_w = {}
        self.rd_c = {}
        self.rd_d = {}

    def add(self, eng, fn, reads=(), writes=(), dma=False):
        oid = len(self.ops)
        deps = set()
        for k in reads:
            w = self.last_w.get(k)
            if w is not None:
                deps.add(w)
        for k in writes:
            w = self.last_w.get(k)
            if w is not None:
                deps.add(w)
            for r in self.rd_c.get(k, {}).values():
                deps.add(r)
            for r in self.rd_d.get(k, ()):
                deps.add(r)
        for k in reads:
            if dma:
                self.rd_d.setdefault(k, []).append(oid)
            else:
                self.rd_c.setdefault(k, {})[eng] = oid
        for k in writes:
            self.last_w[k] = oid
            self.rd_c[k] = {}
            self.rd_d[k] = []
        deps.discard(oid)
        self.ops.append(dict(id=oid, eng=eng, fn=fn, deps=deps, dma=dma, sig=None))
        return oid

    def pe(self, fn, r=(), w=()):
        return self.add("pe", fn, r, w)

    def act(self, fn, r=(), w=()):
        return self.add("act", fn, r, w)

    def dve(self, fn, r=(), w=()):
        return self.add("dve", fn, r, w)

    def dma(self, q, fn, r=(), w=()):
        return self.add(q, fn, r, w, dma=True)

    def emit(self):
        nc = self.nc
        ops = self.ops
        if not ops:
            return
        need_sig = [False] * len(ops)
        for op in ops:
            if op["dma"]:
                need_sig[op["id"]] = True
            for d in op["deps"]:
                p = ops[d]
                if p["dma"]:
                    continue
                if p["eng"] == op["eng"] and not op["dma"]:
                    if p["eng"] == "pe" or not SAME_ENG_SYNC:
                        continue
                need_sig[d] = True
        engs = {}
        for op in ops:
            engs.setdefault(op["eng"], []).append(op)
        for e, lst in engs.items():
            for op in reversed(lst):
                if not op["dma"]:
                    need_sig[op["id"]] = True
                    break
        stack = nc.cleanup_on_exit()
        sems = {}
        cnt = {e: 0 for e in COMPUTE}
        NSLOT = 6
        slot_rr = {}
        slot_uses = {}
        for op in ops:
            if op["dma"]:
                q = op["eng"]
                i = slot_rr.get(q, 0)
                slot_rr[q] = (i + 1) % NSLOT
                nm = f"d_{q}_{i}"
                u = slot_uses.get(nm, 0) + 1
                slot_uses[nm] = u
                op["sig"] = (nm, 16 * u)
                op["slot_prev"] = (nm, 16 * (u - 1))
            elif need_sig[op["id"]]:
                e = op["eng"]
                cnt[e] += 1
                op["sig"] = (f"c_{e}", cnt[e])
        last_sig = {}
        for op in ops:
            if op["sig"] is not None:
                nm = op["sig"][0]
                last_sig[nm] = max(last_sig.get(nm, 0), op["sig"][1])

        with stack:
          for nm in last_sig:
            Sched.PH += 1
            sems[nm] = nc.alloc_semaphore(name=f"{nm}_{Sched.PH}")
          with nc.Block() as block:
              def run_engine(e, eobj):
                  waited = {}

                  def wait(nm, val):
                      if val <= 0 or waited.get(nm, 0) >= val:
                          return
                      eobj.wait_ge(sems[nm], val)
                      waited[nm] = val

                  for op in engs.get(e, []):
                      need = {}
                      for d in op["deps"]:
                          p = ops[d]
                          if (not p["dma"]) and p["eng"] == e and not op["dma"]:
                              if e == "pe" or not SAME_ENG_SYNC:
                                  continue
                          nm, val = p["sig"]
                          need[nm] = max(need.get(nm, 0), val)
                      if op["dma"]:
                          nm, val = op["slot_prev"]
                          need[nm] = max(need.get(nm, 0), val)
                      for nm, val in need.items():
                          wait(nm, val)
                      ins = op["fn"](eobj)
                      if op["sig"] is not None:
                          ins.then_inc(sems[op["sig"][0]], 16 if op["dma"] else 1)
                  if e == "sp":
                      for nm, val in last_sig.items():
                          wait(nm, val)

              if "pe" in engs:
                  @block.tensor
                  def _(eng):
                      run_engine("pe", eng)
              if "act" in engs:
                  @block.scalar
                  def _(eng):
                      run_engine("act", eng)
              if "dve" in engs:
                  @block.vector
                  def _(eng):
                      run_engine("dve", eng)
              if "pool" in engs:
                  @block.gpsimd
                  def _(eng):
                      run_engine("pool", eng)

              @block.sync
              def _(eng):
                  run_engine("sp", eng)


class Cfg:
    def __init__(self, D=1024, R=32, CTX=256, DFF=4096, L=2):
        self.D, self.R, self.CTX, self.DFF, self.L = D, R, CTX, DFF, L
        self.KC = D // 128
        self.NL = R * 64
        self.NT = CTX + self.NL
        self.TT = self.NT // 128
        self.CT = CTX // 128
        self.LT = self.NL // 128
        self.FB = DFF // 512


def tok_blocks(c, with_ctx=True):
    out = []
    if with_ctx:
        s = 0
        while s < c.CTX:
            n = min(512, c.CTX - s)
            out.append((s, n, True))
            s += n
    s = c.CTX
    while s < c.NT:
        n = min(512, c.NT - s)
        out.append((s, n, False))
        s += n
    return out


def build(c, stop_after=None):
    nc = bass.Bass("TRN2", target_bir_lowering=False)
    D, KC, NL, NT, TT, CT, LT, CTX, L, DFF = c.D, c.KC, c.NL, c.NT, c.TT, c.CT, c.LT, c.CTX, c.L, c.DFF

    def din(name, shape):
        return nc.dram_tensor(name, list(shape), F32, kind="ExternalInput").ap()

    x_in = din("x_in", [NL, D])
    ctx_in = din("ctx_in", [CTX, D])
    cc = din("cc", [2, D])
    w1 = din("w1", [L, D, W1C])
    wbr = din("wbr", [L, 3, 512, D])
    wout = din("wout", [L, D, D])
    wm1 = din("wm1", [L, D, DFF])
    wm2 = din("wm2", [L, DFF, D])
    wmod = din("wmod", [L, D, 6 * D])
    bmod = din("bmod", [L, 6 * D])
    gvec = din("gvec", [L, 4, D])
    rdl = din("rdl", [L, 8])
    nab = din("nab", [L, 64, 8 * 17 * 64])
    gqn = din("gqn", [L, 2, 128])
    ident = din("ident", [128, 128])
    perm = din("perm", [128, 128])
    bones = din("bones", [128, 128])
    cosd = din("cosd", [128, NL])
    sind = din("sind", [128, NL])
    ctab = din("ctab", [4, 128, 128])
    posq = din("posq", [128, 128])
    posk = din("posk", [128, 8])
    out = nc.dram_tensor("out", [NL, D], F32, kind="ExternalOutput").ap()
    xs = nc.dram_tensor("xs", [NT, D], F32).ap()
    modd = nc.dram_tensor("modd", [L, 2, 6 * D], F32).ap()

    def w1v(l, c0, n):
        return w1[l].rearrange("(k p) n -> p k n", p=128)[:, :, c0:c0 + n]

    glob = contextlib.ExitStack()

    uid = [0]

    def alloc(stack, name, shape, dt):
        uid[0] += 1
        return stack.enter_context(nc.sbuf_tensor(f"{name}_{uid[0]}", list(shape), dt))

    def palloc(stack, name, shape, dt):
        uid[0] += 1
        return stack.enter_context(nc.psum_tensor(f"{name}_{uid[0]}", list(shape), dt))

    with glob:
        identb = alloc(glob, "identb", [128, 128], BF16)
        permb = alloc(glob, "permb", [128, 128], BF16)
        bonesb = alloc(glob, "bonesb", [128, 128], BF16)
        hT = alloc(glob, "hT", [128, KC, NT], BF16)
        ppar = alloc(glob, "ppar", [128, L * 2 * 2 * 2 * KC], F32)

        def pp(l, s, j, a):
            o = (((l * 2 + s) * 2 + j) * 2 + a) * KC
            return ppar[:, o:o + KC]

        scTb = alloc(glob, "scTb", [128, KC, 2], BF16)
        Pw = alloc(glob, "Pw", [128, KC, 512], BF16)

        def pp_params(S, ph, l):
            gpp = alloc(ph, f"gpp{l}", [128, 2, KC], F32)
            S.dma("act", lambda e: e.dma_start(
                out=gpp[:, 0, :], in_=gvec[l, 0].rearrange("(k p) -> p k", p=128), allow_slow_non_contiguous=True),
                w=[("gpp", l, 0)])
            S.dma("act", lambda e: e.dma_start(
                out=gpp[:, 1, :], in_=gvec[l, 2].rearrange("(k p) -> p k", p=128), allow_slow_non_contiguous=True),
                w=[("gpp", l, 1)])
            for s_ in range(2):
                for j in range(2):
                    sct = alloc(ph, f"sct{l}{s_}{j}", [128, KC], F32)
                    S.dma("act", lambda e, s_=s_, j=j, sct=sct: e.dma_start(
                        out=sct[:], in_=modd[l, s_, (3 * j + 1) * D:(3 * j + 2) * D].rearrange("(k p) -> p k", p=128),
                        allow_slow_non_contiguous=True), r=[("modd", l)], w=[("sct", l, s_, j)])
                    S.dma("act", lambda e, s_=s_, j=j: e.dma_start(
                        out=pp(l, s_, j, 1), in_=modd[l, s_, (3 * j) * D:(3 * j + 1) * D].rearrange("(k p) -> p k", p=128),
                        allow_slow_non_contiguous=True), r=[("modd", l)], w=[("ppall", l, s_, j, 1)])
                    S.dve(lambda e, s_=s_, j=j, sct=sct: e.scalar_tensor_tensor(
                        out=pp(l, s_, j, 0), in0=sct[:], scalar=1.0, in1=gpp[:, j, :], op0=ALU.add, op1=ALU.mult),
                        r=[("sct", l, s_, j), ("gpp", l, j)], w=[("ppall", l, s_, j, 0)])

        with contextlib.ExitStack() as ph:
            S = Sched(nc)
            S.dma("pool", lambda e: e.dma_start(out=identb[:], in_=ident), w=["identb"])
            S.dma("pool", lambda e: e.dma_start(out=permb[:], in_=perm), w=["permb"])
            S.dma("pool", lambda e: e.dma_start(out=bonesb[:], in_=bones), w=["bonesb"])
            ccT = alloc(ph, "ccT", [128, KC, 2], F32)
            scT = alloc(ph, "scT", [128, KC, 2], F32)
            for a in range(2):
                S.dma("sp", lambda e, a=a: e.dma_start(out=ccT[:, :, a], in_=cc[a].rearrange("(k p) -> p k", p=128),
                                                       allow_slow_non_contiguous=True), w=[("ccT", a)])
            S.act(lambda e: e.activation(out=scT[:], in_=ccT[:], func=AF.Silu), r=[("ccT", 0), ("ccT", 1)], w=["scT"])
            modsb = alloc(ph, "modsb", [2, 6 * D], F32)
            bsb = alloc(ph, "bsb", [2, 6 * D], F32)
            wmb = [alloc(ph, f"wmb{i}", [128, KC, 512], F32) for i in range(4)]
            pm = [palloc(ph, f"pm{i}", [128, 512], F32) for i in range(2)]
            nblk = 6 * D // 512
            it = 0
            S.act(lambda e: e.activation(out=scTb[:], in_=scT[:], func=AF.Copy), r=["scT"], w=["scTb"])
            for l in range(1):
                for a in range(2):
                    S.dma("sp", lambda e, l=l, a=a: e.dma_start(out=bsb[a:a + 1, :], in_=bmod[l:l + 1, :]),
                          w=[("bsb", a)])
                for j in range(nblk):
                    b = it % 4
                    it += 1
                    S.dma("sp", lambda e, l=l, j=j, b=b: e.dma_start(
                        out=wmb[b][:], in_=wmod[l].rearrange("(k p) n -> p k n", p=128)[:, :, j * 512:(j + 1) * 512]),
                        w=[("wmb", b)])
                    for k in range(KC):
                        S.pe(lambda e, b=b, k=k: e.matmul(pm[b % 2][0:2, :], lhsT=scT[:, k, :], rhs=wmb[b][:, k, :],
                                                           start=(k == 0), stop=(k == KC - 1)),
                             r=["scT", ("wmb", b)], w=[("pm", b % 2)])
                    S.dve(lambda e, b=b, j=j: e.tensor_tensor(out=modsb[:, j * 512:(j + 1) * 512], in0=pm[b % 2][0:2, :],
                                                              in1=bsb[:, j * 512:(j + 1) * 512], op=ALU.add),
                          r=[("pm", b % 2), ("bsb", 0), ("bsb", 1)], w=["modsb"])
                S.dma("sp", lambda e, l=l: e.dma_start(out=modd[l], in_=modsb[:]), r=["modsb"], w=[("modd", l)])
            pp_params(S, ph, 0)
            S.emit()
        if stop_after == "mod":
            return nc

        def make_hT_ops(S, t, xt, l, j, tmp, pT):
            s = 0 if t < CT else 1
            s = 1 - s
            xk, junk, ssv, rst, xn = tmp
            q = id(xn)
            S.act(lambda e: e.activation(out=junk[:], in_=xt[:], func=AF.Square, accum_out=ssv[:]),
                  r=[xk], w=[("junk", q), ("ssv", q)])
            S.act(lambda e: e.activation(out=rst[:], in_=ssv[:], func=AF.Sqrt, scale=1.0 / D, bias=EPS),
                  r=[("ssv", q)], w=[("rst", q)])
            S.dve(lambda e: e.reciprocal(out=rst[:], in_=rst[:]), r=[("rst", q)], w=[("rst", q)])
            S.act(lambda e: e.activation(out=xn[:], in_=xt[:], func=AF.Copy, scale=rst[:]),
                  r=[xk, ("rst", q)], w=[("xn", q)])
            for k in range(KC):
                S.pe(lambda e, k=k: e.transpose(out=pT[:, k * 128:(k + 1) * 128], in_=xn[:, k * 128:(k + 1) * 128],
                                                identity=identb[:]), r=[("xn", q), "identb"], w=[("pT", id(pT))])
            for k in range(KC):
                S.dve(lambda e, k=k: e.tensor_scalar(
                    out=hT[:, k, t * 128:(t + 1) * 128], in0=pT[:, k * 128:(k + 1) * 128],
                    scalar1=pp(l, s, j, 0)[:, k:k + 1], scalar2=pp(l, s, j, 1)[:, k:k + 1],
                    op0=ALU.mult, op1=ALU.add), r=[("pT", id(pT))], w=[("hT", t)])

        def htmp(ph, S, n=""):
            junk = alloc(ph, "junk" + n, [128, D], F32)
            ssv = alloc(ph, "ssv" + n, [128, 1], F32)
            rst = alloc(ph, "rst" + n, [128, 1], F32)
            xn = alloc(ph, "xn" + n, [128, D], BF16)
            return junk, ssv, rst, xn

        def x_tile_src(t, from_inputs):
            if not from_inputs:
                return xs[t * 128:(t + 1) * 128, :]
            if t < CT:
                return ctx_in[t * 128:(t + 1) * 128, :]
            return x_in[(t - CT) * 128:(t - CT + 1) * 128, :]

        def norm_pipeline(S, ph, tiles, src, gg, l_next, j_next, store, do_hT, tag, from_inputs=False):
            xts = [alloc(ph, f"{tag}xt{i}", [128, D], F32) for i in range(2)]
            tts = [alloc(ph, f"{tag}tt{i}", [128, D], F32) for i in range(2)]
            jk = [alloc(ph, f"{tag}jk{i}", [128, D], F32) for i in range(2)]
            xns = [alloc(ph, f"{tag}xn{i}", [128, D], BF16) for i in range(2)]
            ssy = [alloc(ph, f"{tag}ssy{i}", [128, 1], F32) for i in range(2)]
            ssv = [alloc(ph, f"{tag}ssv{i}", [128, 1], F32) for i in range(2)]
            pTs = [[palloc(ph, f"{tag}pT{i}{par}", [128, 1024], BF16) for par in range(2)] for i in range(2)]
            srcs = {}
            act_ks = [k for k in range(KC) if k % 3 == 1][:3]
            kmap = {}
            for k in range(KC):
                if k in act_ks:
                    kmap[k] = (1, act_ks.index(k))
                else:
                    kmap[k] = (0, len([x for x in range(k) if x not in act_ks]))

            def s1a(ti):
                t = tiles[ti]
                b = ti % 2
                if src is not None:
                    srcs[ti] = src(ti, t, b)

            def s1(ti):
                t = tiles[ti]
                b = ti % 2
                S.dma("sp", lambda e: e.dma_start(out=xts[b][:], in_=x_tile_src(t, from_inputs)), w=[("xt", b)])
                if src is not None:
                    yap, ykeys = srcs[ti]
                    S.act(lambda e: e.activation(out=jk[0][:], in_=yap, func=AF.Square, accum_out=ssy[b][:]),
                          r=ykeys, w=[("jk", 0), ("ssy", b)])
                    S.act(lambda e: e.activation(out=ssy[b][:], in_=ssy[b][:], func=AF.Sqrt, scale=1.0 / D, bias=EPS),
                          r=[("ssy", b)], w=[("ssy", b)])

            def s2(ti):
                t = tiles[ti]
                b = ti % 2
                sidx = 1 if t < CT else 0
                if src is not None:
                    yap, ykeys = srcs[ti]
                    S.dve(lambda e: e.reciprocal(out=ssy[b][:], in_=ssy[b][:]), r=[("ssy", b)], w=[("ssy", b)])
                    S.dve(lambda e: e.scalar_tensor_tensor(out=tts[b][:], in0=yap, scalar=ssy[b][:, 0:1], in1=gg[:, sidx, :],
                                                           op0=ALU.mult, op1=ALU.mult),
                          r=list(ykeys) + [("ssy", b), ("gg", sidx)], w=[("tt", b)])
                    S.dve(lambda e: e.tensor_tensor(out=xts[b][:], in0=xts[b][:], in1=tts[b][:], op=ALU.add),
                          r=[("xt", b), ("tt", b)], w=[("xt", b)])
                    store(S, t, xts[b], ("xt", b))
                if do_hT:
                    S.act(lambda e: e.activation(out=jk[1][:], in_=xts[b][:], func=AF.Square, accum_out=ssv[b][:]),
                          r=[("xt", b)], w=[("jk", 1), ("ssv", b)])
                    S.act(lambda e: e.activation(out=ssv[b][:], in_=ssv[b][:], func=AF.Sqrt, scale=1.0 / D, bias=EPS),
                          r=[("ssv", b)], w=[("ssv", b)])

            def s3(ti):
                b = ti % 2
                if not do_hT:
                    return
                S.dve(lambda e: e.reciprocal(out=ssv[b][:], in_=ssv[b][:]), r=[("ssv", b)], w=[("ssv", b)])
                S.act(lambda e: e.activation(out=xns[b][:], in_=xts[b][:], func=AF.Copy, scale=ssv[b][:]),
                      r=[("xt", b), ("ssv", b)], w=[("xn", b)])
                for k in range(KC):
                    par, pos = kmap[k]
                    S.pe(lambda e, k=k, par=par, pos=pos: e.transpose(
                        out=pTs[b][par][:, pos * 128:(pos + 1) * 128],
                        in_=xns[b][:, k * 128:(k + 1) * 128], identity=identb[:]),
                        r=[("xn", b)], w=[("pT", b, par)])

            def s4(ti):
                t = tiles[ti]
                b = ti % 2
                if not do_hT:
                    return
                sidx = 1 if t < CT else 0
                for k in range(KC):
                    par, pos = kmap[k]
                    if par == 0:
                        S.dve(lambda e, k=k, pos=pos: e.tensor_scalar(
                            out=hT[:, k, t * 128:(t + 1) * 128], in0=pTs[b][0][:, pos * 128:(pos + 1) * 128],
                            scalar1=pp(l_next, sidx, j_next, 0)[:, k:k + 1], scalar2=pp(l_next, sidx, j_next, 1)[:, k:k + 1],
                            op0=ALU.mult, op1=ALU.add),
                            r=[("pT", b, 0), ("ppall", l_next, sidx, j_next, 0), ("ppall", l_next, sidx, j_next, 1)],
                            w=[("hT", t, k)])
                    else:
                        S.act(lambda e, k=k, pos=pos: e.activation(
                            out=hT[:, k, t * 128:(t + 1) * 128], in_=pTs[b][1][:, pos * 128:(pos + 1) * 128],
                            func=AF.Identity,
                            scale=pp(l_next, sidx, j_next, 0)[:, k:k + 1], bias=pp(l_next, sidx, j_next, 1)[:, k:k + 1]),
                            r=[("pT", b, 1), ("ppall", l_next, sidx, j_next, 0), ("ppall", l_next, sidx, j_next, 1)],
                            w=[("hT", t, k)])

            n = len(tiles)
            stages = (s1, s2, s3, s4)
            for step in range(n + 3):
                if step < n:
                    s1a(step)
                for si in (3, 2, 1, 0):
                    i = step - si
                    if 0 <= i < n:
                        stages[si](i)

        with contextlib.ExitStack() as ph:
            S = Sched(nc)
            S.dma("pool", lambda e: e.dma_start(out=Pw[:], in_=w1v(0, RQD, 512)), w=["Pw"])
            norm_pipeline(S, ph, list(range(TT)), None, None, 0, 0, None, True, "i", from_inputs=True)
            S.emit()

        for l in range(L):
            last = (l == L - 1)
            qtiles = list(range(CT, TT)) if last else list(range(TT))
            with contextlib.ExitStack() as mix:
                yT = alloc(mix, "yT", [128, 3, 4, NT], BF16)
                with contextlib.ExitStack() as ret:
                    rqT = alloc(ret, "rqT", [128, 4, NT], BF16)
                    rkT = alloc(ret, "rkT", [128, 2, NT], BF16)
                    rv = alloc(ret, "rv", [128, TT, 512], BF16)
                    DT = alloc(ret, "DT", [128, 4, 128], F32)
                    decq = alloc(ret, "decq", [128, 4, 128], F32)
                    deckf = alloc(ret, "deckf", [128, 4, 2, 64], F32)
                    decsf = alloc(ret, "decsf", [128, 512], F32)
                    with contextlib.ExitStack() as ph:
                        S = Sched(nc)
                        cosT = alloc(ph, "cosT", [128, NL], BF16)
                        sinT = alloc(ph, "sinT", [128, NL], BF16)
                        S.dma("pool", lambda e: e.dma_start(out=cosT[:], in_=cosd), w=["cosT"])
                        S.dma("pool", lambda e: e.dma_start(out=sinT[:], in_=sind), w=["sinT"])
                        ws = [Pw] + [alloc(ph, f"ws{i}", [128, KC, 512], BF16) for i in range(1, 3)]
                        pq = [palloc(ph, f"pq{i}", [128, 512], F32) for i in range(3)]
                        pr = [palloc(ph, f"pr{i}", [128, 512], F32) for i in range(2)]
                        qb = [alloc(ph, f"qb{i}", [128, 512], BF16) for i in range(4)]
                        t1 = [alloc(ph, f"t1{i}", [128, 512], BF16) for i in range(2)]
                        t2 = [alloc(ph, f"t2{i}", [128, 512], BF16) for i in range(2)]
                        cnt = [0, 0, 0]

                        rjobs = []

                        def rope_proj(S, wsb, col, dest_fn, dk):
                            for (s0, n, isc) in tok_blocks(c):
                                rjobs.append((wsb, col, dest_fn, dk, s0, n, isc))

                        def r_p1(i):
                            wsb, col, dest_fn, dk, s0, n, isc = rjobs[i]
                            b = i % 3
                            for k in range(KC):
                                S.pe(lambda e, k=k: e.matmul(
                                    pq[b][:, 0:n], lhsT=ws[wsb][:, k, col:col + 128], rhs=hT[:, k, s0:s0 + n],
                                    start=(k == 0), stop=(k == KC - 1)), r=[("ws", wsb)], w=[("pq", b)])
                            if isc:
                                S.act(lambda e: e.activation(out=dest_fn(s0, n), in_=pq[b][:, 0:n], func=AF.Copy),
                                      r=[("pq", b)], w=[(dk, i)])
                            else:
                                b4 = i % 4
                                S.act(lambda e: e.activation(out=qb[b4][:, 0:n], in_=pq[b][:, 0:n], func=AF.Copy),
                                      r=[("pq", b)], w=[("qb", b4)])

                        def r_pB(i):
                            wsb, col, dest_fn, dk, s0, n, isc = rjobs[i]
                            if isc:
                                return
                            b2, b4 = i % 2, i % 4
                            S.pe(lambda e: e.matmul(pr[b2][:, 0:n], lhsT=permb[:], rhs=qb[b4][:, 0:n], start=True, stop=True),
                                 r=[("qb", b4), "permb"], w=[("pr", b2)])

                        def r_p2(i):
                            wsb, col, dest_fn, dk, s0, n, isc = rjobs[i]
                            if isc:
                                return
                            b2, b4 = i % 2, i % 4
                            l0 = s0 - CTX
                            S.dve(lambda e: e.tensor_tensor(out=t1[b2][:, 0:n], in0=qb[b4][:, 0:n], in1=cosT[:, l0:l0 + n],
                                                            op=ALU.mult), r=[("qb", b4), "cosT"], w=[("t1", b2)])
                            S.dve(lambda e: e.tensor_tensor(out=t2[b2][:, 0:n], in0=pr[b2][:, 0:n], in1=sinT[:, l0:l0 + n],
                                                            op=ALU.mult), r=[("pr", b2), "sinT"], w=[("t2", b2)])
                            S.dve(lambda e: e.tensor_tensor(out=dest_fn(s0, n), in0=t1[b2][:, 0:n], in1=t2[b2][:, 0:n],
                                                            op=ALU.add), r=[("t1", b2), ("t2", b2)], w=[(dk, i)])

                        S.dma("pool", lambda e: e.dma_start(out=ws[1][:, :, 0:256], in_=w1v(l, RK, 256)), w=[("ws", 1)])
                        S.dma("pool", lambda e: e.dma_start(out=ws[2][:], in_=w1v(l, RV, 512)), w=[("ws", 2)])
                        for h in range(4):
                            rope_proj(S, 0, h * 128, lambda s0, n, h=h: rqT[:, h, s0:s0 + n], "rqT")
                        for j in range(2):
                            rope_proj(S, 1, j * 128, lambda s0, n, j=j: rkT[:, j, s0:s0 + n], "rkT")
                        for step in range(len(rjobs) + 3):
                            if 0 <= step - 3 < len(rjobs):
                                r_p2(step - 3)
                            if 0 <= step - 2 < len(rjobs):
                                r_pB(step - 2)
                            if step < len(rjobs):
                                r_p1(step)
                        lgr = alloc(ph, "lgr", [128, 8], F32)
                        lg = alloc(ph, "lg", [128, 8], F32)
                        lgq = alloc(ph, "lgq", [128, 4], F32)
                        decs = alloc(ph, "decs", [128, 4], F32)
                        deck = alloc(ph, "deck", [128, 8], F32)
                        poskt = alloc(ph, "poskt", [128, 8], F32)
                        posqt = alloc(ph, "posqt", [128, 128], F32)
                        ctt = alloc(ph, "ctt", [128, 4, 128], F32)
                        dtmp = alloc(ph, "dtmp", [128, 2, 128], F32)
                        S.dma("sp", lambda e: e.dma_start(out=lgr[:], in_=rdl[l:l + 1, :].partition_broadcast(128)),
                              w=["lgr"])
                        S.dma("sp", lambda e: e.dma_start(out=poskt[:], in_=posk), w=["poskt"])
                        S.dma("sp", lambda e: e.dma_start(out=posqt[:], in_=posq), w=["posqt"])
                        S.dma("sp", lambda e: e.dma_start(out=ctt[:], in_=ctab.rearrange("a p n -> p a n")), w=["ctt"])
                        S.act(lambda e: e.activation(out=lg[:], in_=lgr[:], func=AF.Exp, scale=-1.0), r=["lgr"], w=["lg"])
                        S.act(lambda e: e.activation(out=lg[:], in_=lg[:], func=AF.Ln, bias=1.0), r=["lg"], w=["lg"])
                        S.dve(lambda e: e.tensor_scalar(out=lg[:], in0=lg[:], scalar1=-1.0, scalar2=None, op0=ALU.mult),
                              r=["lg"], w=["lg"])
                        S.dve(lambda e: e.tensor_copy(out=lgq[0:64, :], in_=lg[0:64, 0:4]), r=["lg"], w=["lgq"])
                        S.dve(lambda e: e.tensor_copy(out=lgq[64:128, :], in_=lg[64:128, 4:8]), r=["lg"], w=["lgq"])
                        S.act(lambda e: e.activation(out=decs[:], in_=lgq[:], func=AF.Exp, scale=128.0),
                              r=["lgq"], w=["decs"])
                        S.dve(lambda e: e.tensor_tensor(out=deck[:], in0=lg[:], in1=poskt[:], op=ALU.mult),
                              r=["lg", "poskt"], w=["deck"])
                        S.act(lambda e: e.activation(out=deck[:], in_=deck[:], func=AF.Exp), r=["deck"], w=["deck"])
                        for h in range(4):
                            S.act(lambda e, h=h: e.activation(out=decq[:, h, :], in_=posqt[:], func=AF.Exp,
                                                              scale=lgq[:, h:h + 1]), r=["posqt", "lgq"], w=["decq"])
                            S.act(lambda e, h=h: e.activation(out=dtmp[:, 0, :], in_=ctt[:, 0, :], func=AF.Exp,
                                                              scale=lg[:, h:h + 1]), r=["ctt", "lg"], w=["dtmp0"])
                            S.act(lambda e, h=h: e.activation(out=dtmp[:, 1, :], in_=ctt[:, 2, :], func=AF.Exp,
                                                              scale=lg[:, 4 + h:5 + h]), r=["ctt", "lg"], w=["dtmp1"])
                            S.dve(lambda e: e.tensor_tensor(out=dtmp[:, 0, :], in0=dtmp[:, 0, :], in1=ctt[:, 1, :],
                                                            op=ALU.mult), r=["dtmp0", "ctt"], w=["dtmp0"])
                            S.dve(lambda e: e.tensor_tensor(out=dtmp[:, 1, :], in0=dtmp[:, 1, :], in1=ctt[:, 3, :],
                                                            op=ALU.mult), r=["dtmp1", "ctt"], w=["dtmp1"])
                            S.dve(lambda e, h=h: e.tensor_tensor(out=DT[:, h, :], in0=dtmp[:, 0, :], in1=dtmp[:, 1, :],
                                                                 op=ALU.add), r=["dtmp0", "dtmp1"], w=["DT"])
                        S.dve(lambda e: e.tensor_scalar(out=DT[:], in0=DT[:], scalar1=0.125, scalar2=None, op0=ALU.mult),
                              r=["DT"], w=["DT"])
                        S.dve(lambda e: e.tensor_scalar(out=decq[:], in0=decq[:], scalar1=0.125, scalar2=None,
                                                        op0=ALU.mult), r=["decq"], w=["decq"])
                        S.dve(lambda e: e.memset(deckf[:], 1.0), w=["deckf"])
                        S.dve(lambda e: e.memset(decsf[:], 1.0), w=["decsf"])
                        for h in range(4):
                            for d in range(2):
                                S.dve(lambda e, h=h, d=d: e.tensor_scalar(
                                    out=deckf[:, h, d, :], in0=deckf[:, h, d, :], scalar1=deck[:, d * 4 + h:d * 4 + h + 1],
                                    scalar2=None, op0=ALU.mult), r=["deckf", "deck"], w=["deckf"])
                            S.dve(lambda e, h=h: e.tensor_scalar(
                                out=decsf[:, h * 128:(h + 1) * 128], in0=decsf[:, h * 128:(h + 1) * 128],
                                scalar1=decs[:, h:h + 1], scalar2=None, op0=ALU.mult), r=["decsf", "decs"], w=["decsf"])
                        cnt[0] = len(rjobs)
                        for t in range(TT):
                            b = cnt[0] % 3
                            cnt[0] += 1
                            for k in range(KC):
                                S.pe(lambda e, b=b, k=k, t=t: e.matmul(
                                    pq[b][:, :], lhsT=hT[:, k, t * 128:(t + 1) * 128], rhs=ws[2][:, k, :],
                                    start=(k == 0), stop=(k == KC - 1)), r=[("ws", 2), ("hT", t)], w=[("pq", b)])
                            S.act(lambda e, b=b, t=t: e.activation(out=rv[:, t, :], in_=pq[b][:, :], func=AF.Copy),
                                  r=[("pq", b)], w=[("rv", t)])
                        S.emit()
                    if stop_after == "retproj":
                        return nc
                    Sall = alloc(ret, "Sall", [128, TT, 512], BF16)
                    wg = alloc(ret, "wg", [128, KC, 512], BF16)
                    with contextlib.ExitStack() as ph:
                        S = Sched(nc)
                        Srun = alloc(ph, "Srun", [128, 512], F32)
                        S.dma("pool", lambda e: e.dma_start(out=wg[:], in_=w1v(l, RG, 512)), w=["wg"])
                        pkt = [palloc(ph, f"pkt{i}", [128, 1024], BF16) for i in range(2)]
                        pds = [palloc(ph, f"pds{i}", [128, 512], F32) for i in range(2)]
                        ktall = alloc(ph, "ktall", [128, TT, 4, 2, 64], BF16)
                        for ch in range(TT):
                            pb = ch % 2
                            for j in range(2):
                                S.pe(lambda e, j=j, ch=ch, pb=pb: e.transpose(
                                    out=pkt[pb][:, j * 128:(j + 1) * 128], in_=rkT[:, j, ch * 128:(ch + 1) * 128],
                                    identity=identb[:]), r=["identb"], w=[("pkt", pb)])
                            for d in range(2):
                                S.dve(lambda e, d=d, ch=ch, pb=pb: e.tensor_tensor(
                                    out=ktall[:, ch, :, d, :], in0=pkt[pb][:, 0:256].rearrange("p (h x) -> p h x", h=4),
                                    in1=deckf[:, :, d, :], op=ALU.mult), r=[("pkt", pb), "deckf"], w=[("kt", ch)])
                        S.dve(lambda e: e.memset(Srun[:], 0.0), w=["Srun0", "Srun1"])
                        fwd_order = list(range(TT))
                        bwd_order = list(range(CT - 1, -1, -1)) + list(range(TT - 1, CT - 1, -1))
                        orders = (fwd_order, bwd_order)
                        for ci in range(TT):
                            for half in range(2):
                                ch = orders[half][ci]
                                p0, p1 = half * 64, half * 64 + 64
                                sk = f"Srun{half}"
                                S.act(lambda e, ch=ch, p0=p0, p1=p1: e.activation(out=Sall[p0:p1, ch, :], in_=Srun[p0:p1, :],
                                                                                  func=AF.Copy),
                                      r=[sk], w=[("Sall", ch, half)])
                                if ci == TT - 1:
                                    continue
                                for h in range(4):
                                    S.pe(lambda e, h=h, ch=ch, half=half: e.matmul(
                                        pds[half][:, h * 128:(h + 1) * 128], lhsT=ktall[:, ch, h, :, :],
                                        rhs=rv[:, ch, h * 128:(h + 1) * 128], start=True, stop=True),
                                        r=[("kt", ch)], w=[("pds", half)])
                                S.dve(lambda e, p0=p0, p1=p1: e.tensor_tensor(
                                    out=Srun[p0:p1, :], in0=Srun[p0:p1, :], in1=decsf[p0:p1, :], op=ALU.mult),
                                    r=[sk, "decsf"], w=[sk])
                                S.dve(lambda e, p0=p0, p1=p1, half=half: e.tensor_tensor(
                                    out=Srun[p0:p1, :], in0=Srun[p0:p1, :], in1=pds[half][p0:p1, :], op=ALU.add),
                                    r=[sk, ("pds", half)], w=[sk])
                        S.emit()
                    ssall = alloc(ret, "ssall", [128, TT * 4], F32)
                    uall = alloc(ret, "uall", [128, TT, 512], BF16)
                    with contextlib.ExitStack() as ph:
                        S = Sched(nc)
                        pa = [[palloc(ph, f"pa{i}{j}", [128, 512], F32) for j in range(2)] for i in range(2)]
                        po = [palloc(ph, f"po{i}", [128, 512], F32) for i in range(2)]
                        pg = [palloc(ph, f"pg{i}", [128, 512], F32) for i in range(2)]
                        osb = [alloc(ph, f"osb{i}", [128, 512], BF16) for i in range(2)]
                        at = [Pw[:, i, :].rearrange("p (h n) -> p h n", h=4) for i in range(2)]
                        qt_ = [Pw[:, 2 + i, :].rearrange("p (h n) -> p h n", h=4) for i in range(2)]
                        sg = [alloc(ph, f"sg{i}", [128, 512], F32) for i in range(2)]
                        junk2 = alloc(ph, "junk2", [128, 128], F32)

                        def stA(ch, b):
                            tk = slice(ch * 128, (ch + 1) * 128)
                            S.dve(lambda e: e.tensor_tensor(out=qt_[b][:], in0=rqT[:, :, tk], in1=decq[:], op=ALU.mult),
                                  w=[("qt_", b)])
                            for h in range(4):
                                hc, hh = h // 2, (h % 2) * 64
                                S.pe(lambda e, h=h, hc=hc, hh=hh: e.matmul(
                                    pa[b][h % 2][:, hc * 128:(hc + 1) * 128], lhsT=rkT[hh:hh + 64, hc, tk],
                                    rhs=rqT[hh:hh + 64, h, tk], start=True, stop=True), w=[("pa", b, h % 2)])
                            for k in range(KC):
                                S.pe(lambda e, k=k: e.matmul(pg[b][:], lhsT=hT[:, k, tk], rhs=wg[:, k, :],
                                                             start=(k == 0), stop=(k == KC - 1)), r=["wg"], w=[("pg", b)])

                        def stB(ch, b, ci):
                            tk = slice(ch * 128, (ch + 1) * 128)
                            for par in range(2):
                                S.dve(lambda e, par=par: e.tensor_tensor(
                                    out=at[b][:, par::2, :], in0=pa[b][par][:, 0:256].rearrange("p (h n) -> p h n", h=2),
                                    in1=DT[:, par::2, :], op=ALU.mult), r=[("pa", b, par)], w=[("at", b)])
                            for h in range(4):
                                S.pe(lambda e, h=h: e.matmul(
                                    po[b][:, h * 128:(h + 1) * 128], lhsT=at[b][:, h, :], rhs=rv[:, ch, h * 128:(h + 1) * 128],
                                    start=True, stop=False), r=[("at", b)], w=[("po", b)])
                                S.pe(lambda e, h=h: e.matmul(
                                    po[b][:, h * 128:(h + 1) * 128], lhsT=qt_[b][:, h, :], rhs=Sall[:, ch, h * 128:(h + 1) * 128],
                                    start=False, stop=True), r=[("qt_", b)], w=[("po", b)])
                            S.act(lambda e: e.activation(out=osb[b][:], in_=po[b][:], func=AF.Copy), r=[("po", b)], w=[("osb", b)])
                            S.act(lambda e: e.activation(out=sg[b][:], in_=pg[b][:], func=AF.Silu), r=[("pg", b)], w=[("sg", b)])
                            for h in range(4):
                                S.dve(lambda e, h=h: e.scalar_tensor_tensor(
                                    out=junk2[:], in0=osb[b][:, h * 128:(h + 1) * 128], scalar=1.0,
                                    in1=osb[b][:, h * 128:(h + 1) * 128], op0=ALU.mult, op1=ALU.mult,
                                    accum_out=ssall[:, ci * 4 + h:ci * 4 + h + 1]),
                                    r=[("osb", b)], w=["junk2", ("ssall", ci)])
                            S.dve(lambda e: e.tensor_tensor(out=uall[:, ci, :], in0=osb[b][:], in1=sg[b][:], op=ALU.mult),
                                  r=[("osb", b), ("sg", b)], w=[("uall", ci)])

                        def stC(ch, b, ci):
                            tk = slice(ch * 128, (ch + 1) * 128)
                            for h in range(4):
                                S.dve(lambda e, h=h: e.tensor_scalar(
                                    out=ytm[b][:, h * 128:(h + 1) * 128], in0=uall[:, ci, h * 128:(h + 1) * 128],
                                    scalar1=ssall[:, ci * 4 + h:ci * 4 + h + 1], scalar2=None, op0=ALU.mult),
                                    r=[("uall", ci), "rstd"], w=[("ytm", b)])
                            for j in range(4):
                                S.pe(lambda e, j=j: e.transpose(out=pyt[b][:, j * 128:(j + 1) * 128],
                                                                in_=ytm[b][:, j * 128:(j + 1) * 128], identity=identb[:]),
                                     r=[("ytm", b)], w=[("pyt", b)])
                            S.act(lambda e: e.activation(out=yT[:, 0, :, tk],
                                                         in_=pyt[b][:, 0:512].rearrange("p (j n) -> p j n", j=4),
                                                         func=AF.Copy), r=[("pyt", b)], w=[("yT0", ci)])

                        nq = len(qtiles)
                        for step in range(nq + 1):
                            if step < nq:
                                stA(qtiles[step], step % 2)
                            if 0 <= step - 1 < nq:
                                stB(qtiles[step - 1], (step - 1) % 2, step - 1)
                        S.emit()
                    with contextlib.ExitStack() as ph:
                        S = Sched(nc)
                        pyt = [palloc(ph, f"pyt{i}", [128, 1024], BF16) for i in range(2)]
                        ytm = [alloc(ph, f"ytmb{i}", [128, 512], BF16) for i in range(2)]
                        S.dma("pool", lambda e: e.dma_start(out=Pw[:], in_=w1v(l, NQ, 512)), w=["Pw"])
                        S.act(lambda e: e.activation(out=ssall[:, 0:nq * 4], in_=ssall[:, 0:nq * 4], func=AF.Sqrt,
                                                     scale=1.0 / 128, bias=EPS),
                              r=[("ssall", ci) for ci in range(nq)], w=["rstd"])
                        S.dve(lambda e: e.reciprocal(out=ssall[:, 0:nq * 4], in_=ssall[:, 0:nq * 4]), r=["rstd"], w=["rstd"])
                        for ci in range(nq):
                            stC(qtiles[ci], ci % 2, ci)
                        S.emit()
                if stop_after == "ret":
                    return nc, yT
                with contextlib.ExitStack() as nast:
                  nqT = alloc(nast, "nqT", [128, 4, NT], BF16)
                  nkT = alloc(nast, "nkT", [128, 8, NT], BF16)
                  nv = alloc(nast, "nv", [128, TT, 8, 65], BF16)
                  with contextlib.ExitStack() as ph:
                    S = Sched(nc)
                    ws = [Pw] + [alloc(ph, f"nws{i}", [128, KC, 512], BF16) for i in range(2)]
                    pq = [palloc(ph, f"npq{i}", [128, 512], F32) for i in range(3)]
                    S.dma("pool", lambda e: e.dma_start(out=ws[1][:], in_=w1v(l, NK, 512)), w=[("ws", 1)])
                    S.dma("pool", lambda e: e.dma_start(out=ws[2][:], in_=w1v(l, NV, 512)), w=[("ws", 2)])
                    S.dve(lambda e: e.memset(nv[:, :, :, 64:65], 1.0), w=["nv1"])
                    S.dve(lambda e: e.memset(nkT[64:128, 0::2, :], 0.0), w=["nkz"])
                    S.dve(lambda e: e.memset(nkT[0:64, 1::2, :], 0.0), w=["nkz"])
                    pcnt = 0
                    for wi in (0, 1):
                        for j in range(4):
                            for (s0, n, isc) in tok_blocks(c):
                                b = pcnt % 3
                                pcnt += 1
                                for k in range(KC):
                                    S.pe(lambda e, b=b, k=k, s0=s0, n=n, wi=wi, j=j: e.matmul(
                                        pq[b][:, 0:n], lhsT=ws[wi][:, k, j * 128:(j + 1) * 128], rhs=hT[:, k, s0:s0 + n],
                                        start=(k == 0), stop=(k == KC - 1)), r=[("ws", wi)], w=[("pq", b)])
                                if wi == 0:
                                    S.act(lambda e, b=b, s0=s0, n=n, j=j: e.activation(
                                        out=nqT[:, j, s0:s0 + n], in_=pq[b][:, 0:n], func=AF.Copy),
                                        r=[("pq", b)], w=[("nqk", 0, j, s0)])
                                else:
                                    S.act(lambda e, b=b, s0=s0, n=n, j=j: e.activation(
                                        out=nkT[0:64, 2 * j, s0:s0 + n], in_=pq[b][0:64, 0:n], func=AF.Copy),
                                        r=[("pq", b), "nkz"], w=[("nqk", 1, j, s0)])
                                    S.dve(lambda e, b=b, s0=s0, n=n, j=j: e.tensor_copy(
                                        out=nkT[64:128, 2 * j + 1, s0:s0 + n], in_=pq[b][64:128, 0:n]),
                                        r=[("pq", b), "nkz"], w=[("nqk", 2, j, s0)])
                    for t in range(TT):
                        b = pcnt % 3
                        pcnt += 1
                        for k in range(KC):
                            S.pe(lambda e, b=b, k=k, t=t: e.matmul(
                                pq[b][:, :], lhsT=hT[:, k, t * 128:(t + 1) * 128], rhs=ws[2][:, k, :],
                                start=(k == 0), stop=(k == KC - 1)), r=[("ws", 2)], w=[("pq", b)])
                        S.act(lambda e, b=b, t=t: e.activation(out=nv[:, t, :, 0:64],
                                                               in_=pq[b][:, :].rearrange("p (h d) -> p h d", h=8),
                                                               func=AF.Copy), r=[("pq", b)], w=[("nv", t)])
                    S.emit()
                  with contextlib.ExitStack() as ph:
                    S = Sched(nc)
                    Et = alloc(ph, "Et", [128, 8, 17, 64], BF16)
                    Ef = [alloc(ph, f"Ef{i}", [128, 17 * 64], F32) for i in range(2)]
                    pq = [palloc(ph, f"npo{i}", [128, 512], F32) for i in range(2)]
                    pw_chunks = list(range(KC))
                    for h in range(8):
                        eb = h % 2
                        for u in range(2):
                            S.dma("sp", lambda e, h=h, eb=eb, u=u: e.dma_start(
                                out=Ef[eb][u * 64:(u + 1) * 64, :], in_=nab[l, :, h * 1088:(h + 1) * 1088]),
                                w=[("Ef", eb, u)])
                        S.act(lambda e, h=h, eb=eb: e.activation(out=Et[:, h, :, :].rearrange("p b c -> p (b c)"),
                                                                 in_=Ef[eb][:], func=AF.Exp),
                              r=[("Ef", eb, 0), ("Ef", eb, 1)], w=[("Et", h)])
                    NB, LA = 4, 3
                    pst = [palloc(ph, f"pst{i}", [128, 512], F32) for i in range(NB)]
                    ppo = pq
                    ppt = palloc(ph, "ppt", [128, 1024], BF16)
                    ptb = [alloc(ph, f"ptb{i}", [128, 4, 128], BF16) for i in range(NB)]
                    ytm = alloc(ph, "nytm", [128, 512], BF16)
                    rc = alloc(ph, "nrc", [128, 8, 1], F32)
                    its = []
                    for qt in qtiles:
                        isq_ctx = qt < CT
                        if isq_ctx:
                            ntl, lt0, R0, r, nrows = 0, 0, 0, 0, 0
                        else:
                            r = (qt - CT) * 2
                            r0a = min(max(r - 4, 0), c.R - 8)
                            r0b = min(max(r + 1 - 4, 0), c.R - 8)
                            R0 = r0a
                            nrows = r0b + 8 - r0a
                            ntl = (nrows + 1) // 2
                            lt0 = R0 // 2
                        blocks = [(t, CT + lt0 + t, True) for t in range(ntl)] + [(None, kt_, False) for kt_ in range(CT)]
                        parts = [blocks[i_:i_ + 4] for i_ in range(0, len(blocks), 4)]
                        for h in range(8):
                            for pi, part in enumerate(parts):
                                its.append(dict(qt=qt, h=h, isq_ctx=isq_ctx, ntl=ntl, R0=R0, r=r, nrows=nrows, part=part,
                                                first=(pi == 0), last=(pi == len(parts) - 1)))

                    def emit_S(i):
                        it = its[i]
                        h = it["h"]
                        hc = h // 2
                        b = i % NB
                        tq = slice(it["qt"] * 128, (it["qt"] + 1) * 128)
                        for j, (t_, ktile, isloc) in enumerate(it["part"]):
                            S.pe(lambda e, j=j, ktile=ktile: e.matmul(
                                pst[b][:, j * 128:(j + 1) * 128],
                                lhsT=nkT[:, h, ktile * 128:(ktile + 1) * 128],
                                rhs=nqT[:, hc, tq], start=True, stop=True),
                                w=[("pst", b)])

                    for i0 in range(min(LA, len(its))):
                        emit_S(i0)
                    for i in range(len(its)):
                        it = its[i]
                        qt, h, isq_ctx, ntl, R0, r, nrows, part = (it[k] for k in
                                                                   ("qt", "h", "isq_ctx", "ntl", "R0", "r", "nrows", "part"))
                        nb = len(part)
                        b = i % NB
                        tq = slice(qt * 128, (qt + 1) * 128)
                        if i + LA < len(its):
                            emit_S(i + LA)
                        if pw_chunks and i >= 6 and (i % 6 == 0 or i == len(its) - 1):
                            for kq in ([pw_chunks.pop(0)] if i < len(its) - 1 else list(pw_chunks)):
                                S.dma("pool", lambda e, kq=kq: e.dma_start(out=Pw[:, kq, :], in_=w1v(l, GQ, 512)[:, kq, :]),
                                      w=[("Pw", kq)])
                            if i == len(its) - 1:
                                pw_chunks = []
                        pkeys = [("ptb", b, u, qo) for u in range(2) for qo in range(2)] + [("ptbz", b, z) for z in range(4)]
                        S.act(lambda e, b=b, nb=nb: e.activation(
                            out=ptb[b][:, 0:nb, :], in_=pst[b][:, 0:nb * 128].rearrange("p (a n) -> p a n", a=nb),
                            func=AF.Exp, scale=0.125), r=[("pst", b)], w=pkeys)
                        loc = [(j, t_) for j, (t_, ktile, isloc) in enumerate(part) if isloc]
                        if loc:
                            zc = 0
                            jl0 = loc[0][0]
                            tl = [t_ for (_, t_) in loc]
                            for u in range(2):
                                for qo in range(2):
                                    qr = r + qo
                                    r0q = min(max(qr - 4, 0), c.R - 8)
                                    valid = [t for t in tl if r0q <= R0 + 2 * t + u <= r0q + 7]
                                    inval = [t for t in tl if t not in valid
                                             and not (u == 1 and t == ntl - 1 and nrows % 2 == 1)]
                                    if valid:
                                        ta, tb_ = valid[0], valid[-1] + 1
                                        assert valid == list(range(ta, tb_))
                                        j0 = R0 + 2 * ta + u - qr + 8
                                        ja = jl0 + (ta - tl[0])
                                        jb = ja + (tb_ - ta)
                                        S.add("pool" if u == 1 else "dve",
                                              lambda e, b=b, u=u, qo=qo, ja=ja, jb=jb, j0=j0, h=h, ta=ta, tb_=tb_: e.tensor_tensor(
                                            out=ptb[b][u * 64:(u + 1) * 64, ja:jb, qo * 64:(qo + 1) * 64],
                                            in0=ptb[b][u * 64:(u + 1) * 64, ja:jb, qo * 64:(qo + 1) * 64],
                                            in1=Et[u * 64:(u + 1) * 64, h, j0:j0 + 2 * (tb_ - ta):2, :], op=ALU.mult),
                                            [("ptb", b, u, qo), ("Et", h)], [("ptb", b, u, qo)])
                                    for t in inval:
                                        jz = jl0 + (t - tl[0])
                                        S.add("pool", lambda e, b=b, u=u, qo=qo, jz=jz: e.memset(
                                            ptb[b][u * 64:(u + 1) * 64, jz, qo * 64:(qo + 1) * 64], 0.0),
                                            (), [("ptbz", b, zc)])
                                        zc += 1
                            assert zc <= 4
                        pb = h // 4
                        for j, (t_, ktile, isloc) in enumerate(part):
                            K = 128
                            if isloc and t_ == ntl - 1 and (nrows % 2 == 1):
                                K = 64
                            S.pe(lambda e, b=b, j=j, ktile=ktile, h=h, pb=pb, K=K, nb=nb, it=it: e.matmul(
                                ppo[pb][:, (h % 4) * 65:(h % 4) * 65 + 65], lhsT=ptb[b][0:K, j, :],
                                rhs=nv[0:K, ktile, h, :], start=(it["first"] and j == 0), stop=(it["last"] and j == nb - 1)),
                                r=pkeys + ["nv", "nv1"], w=[("pq", pb)])
                        if h in (3, 7) and it["last"]:
                            for pb in (h // 4,):
                                S.dve(lambda e, pb=pb: e.reciprocal(
                                    out=rc[:, pb * 4:(pb + 1) * 4, :],
                                    in_=ppo[pb][:, 0:260].rearrange("p (h d) -> p h d", h=4)[:, :, 64:65]),
                                    r=[("pq", pb)], w=[("rc", pb)])
                                for hq in range(4):
                                    h2 = pb * 4 + hq
                                    S.dve(lambda e, pb=pb, hq=hq, h2=h2: e.tensor_scalar(
                                        out=ytm[:, h2 * 64:(h2 + 1) * 64], in0=ppo[pb][:, hq * 65:hq * 65 + 64],
                                        scalar1=rc[:, h2, :], scalar2=None, op0=ALU.mult),
                                        r=[("pq", pb), ("rc", pb)], w=["ytm"])
                        if h == 7 and it["last"]:
                            for j in range(4):
                                S.pe(lambda e, j=j: e.transpose(out=ppt[:, j * 128:(j + 1) * 128],
                                                                in_=ytm[:, j * 128:(j + 1) * 128], identity=identb[:]),
                                     r=["ytm"], w=["ppt"])
                            S.act(lambda e, tq=tq: e.activation(out=yT[:, 1, :, tq],
                                                                in_=ppt[:, 0:512].rearrange("p (j n) -> p j n", j=4),
                                                                func=AF.Copy), r=["ppt"], w=[("yT1", qt)])
                    S.emit()
                if stop_after == "na":
                    return nc, yT
                with contextlib.ExitStack() as ph:
                    S = Sched(nc)
                    gqT = alloc(ph, "gqT", [128, 4, NT], BF16)
                    gkT = alloc(ph, "gkT", [128, 2, NT], BF16)
                    gvA = alloc(ph, "gvA", [128, TT, 2, 128], BF16)
                    gvB = alloc(ph, "gvB", [128, TT, 2, 128], BF16)
                    cosT = alloc(ph, "gcosT", [128, NL], BF16)
                    sinT = alloc(ph, "gsinT", [128, NL], BF16)
                    gain = alloc(ph, "gain", [128, 2], F32)
                    S.dma("pool", lambda e: e.dma_start(out=cosT[:], in_=cosd), w=["cosT"])
                    S.dma("pool", lambda e: e.dma_start(out=sinT[:], in_=sind), w=["sinT"])
                    S.dma("sp", lambda e: e.dma_start(out=gain[:], in_=gqn[l].rearrange("a p -> p a"),
                                                      allow_slow_non_contiguous=True), w=["gain"])
                    ws = [Pw, alloc(ph, "gws1", [128, KC, 512], BF16)]
                    S.dma("pool", lambda e: e.dma_start(out=ws[1][:, :, 0:256], in_=w1v(l, GK, 256)), w=[("ws", 1)])
                    S.dve(lambda e: e.memset(gvA[:, :, :, 64:128], 0.0), w=["gv1"])
                    S.dve(lambda e: e.memset(gvA[:, :, :, 64:65], 1.0), w=["gv1"])
                    S.dve(lambda e: e.memset(gkT[64:128, 0, :], 0.0), w=["gkz"])
                    S.dve(lambda e: e.memset(gkT[0:64, 1, :], 0.0), w=["gkz"])
                    S.dve(lambda e: e.memset(gvB[:, :, :, 0:64], 0.0), w=["gv1"])
                    S.dve(lambda e: e.memset(gvB[:, :, :, 0:1], 1.0), w=["gv1"])
                    pq = [palloc(ph, f"gpq{i}", [128, 512], F32) for i in range(2)]
                    pr = [palloc(ph, f"gpr{i}", [128, 512], F32) for i in range(2)]
                    sq = [alloc(ph, f"gsq{i}", [128, 512], BF16) for i in range(2)]
                    rs = [alloc(ph, f"grs{i}", [128, 512], F32) for i in range(2)]
                    qn = [alloc(ph, f"gqn{i}", [128, 512], BF16) for i in range(2)]
                    t1 = [alloc(ph, f"gt1{i}", [128, 512], BF16) for i in range(2)]
                    t2 = [alloc(ph, f"gt2{i}", [128, 512], BF16) for i in range(2)]
                    gc = 0
                    jobs = [(0, j * 128, (lambda s0, n, j=j: gqT[:, j, s0:s0 + n]), 0, not last) for j in range(4)]
                    jobs.append((1, 0, None, 1, True))
                    gjobs = []
                    for (wi, col, dest_fn, gi, wctx) in jobs:
                        for (s0, n, isc) in tok_blocks(c, with_ctx=wctx):
                            gjobs.append((wi, col, dest_fn, gi, s0, n, isc))

                    def g_p1(i):
                        wi, col, dest_fn, gi, s0, n, isc = gjobs[i]
                        b = i % 2
                        for k in range(KC):
                            S.pe(lambda e, k=k: e.matmul(
                                pq[b][:, 0:n], lhsT=ws[wi][:, k, col:col + 128], rhs=hT[:, k, s0:s0 + n],
                                start=(k == 0), stop=(k == KC - 1)), r=[("ws", wi)], w=[("pq", b)])
                        S.act(lambda e: e.activation(out=sq[b][:, 0:n], in_=pq[b][:, 0:n], func=AF.Square),
                              r=[("pq", b)], w=[("sq", b)])

                    def g_p2(i):
                        wi, col, dest_fn, gi, s0, n, isc = gjobs[i]
                        b = i % 2
                        S.pe(lambda e: e.matmul(pr[b][:, 0:n], lhsT=bonesb[:], rhs=sq[b][:, 0:n], start=True, stop=True),
                             r=[("sq", b)], w=[("pr", b)])
                        S.act(lambda e: e.activation(out=rs[b][:, 0:n], in_=pr[b][:, 0:n], func=AF.Ln, scale=1.0 / 64, bias=EPS),
                              r=[("pr", b)], w=[("rs", b)])
                        S.act(lambda e: e.activation(out=rs[b][:, 0:n], in_=rs[b][:, 0:n], func=AF.Exp, scale=-0.5),
                              r=[("rs", b)], w=[("rs", b)])
                        if isc and dest_fn is None:
                            for hf in range(2):
                                S.dve(lambda e, hf=hf: e.scalar_tensor_tensor(
                                    out=gkT[hf * 64:(hf + 1) * 64, hf, s0:s0 + n], in0=pq[b][hf * 64:(hf + 1) * 64, 0:n],
                                    scalar=gain[hf * 64:(hf + 1) * 64, gi:gi + 1], in1=rs[b][hf * 64:(hf + 1) * 64, 0:n],
                                    op0=ALU.mult, op1=ALU.mult), r=[("pq", b), ("rs", b), "gain", "gkz"], w=[("gqk", i)])
                        elif isc:
                            S.dve(lambda e: e.scalar_tensor_tensor(
                                out=dest_fn(s0, n), in0=pq[b][:, 0:n], scalar=gain[:, gi:gi + 1], in1=rs[b][:, 0:n],
                                op0=ALU.mult, op1=ALU.mult), r=[("pq", b), ("rs", b), "gain"], w=[("gqk", i)])
                        else:
                            S.dve(lambda e: e.scalar_tensor_tensor(
                                out=qn[b][:, 0:n], in0=pq[b][:, 0:n], scalar=gain[:, gi:gi + 1], in1=rs[b][:, 0:n],
                                op0=ALU.mult, op1=ALU.mult), r=[("pq", b), ("rs", b), "gain"], w=[("qn", b)])

                    def g_p3(i):
                        wi, col, dest_fn, gi, s0, n, isc = gjobs[i]
                        b = i % 2
                        if isc:
                            return
                        l0 = s0 - CTX
                        S.pe(lambda e: e.matmul(pr[b][:, 0:n], lhsT=permb[:], rhs=qn[b][:, 0:n], start=True, stop=True),
                             r=[("qn", b)], w=[("pr", b)])
                        S.dve(lambda e: e.tensor_tensor(out=t1[b][:, 0:n], in0=qn[b][:, 0:n], in1=cosT[:, l0:l0 + n],
                                                        op=ALU.mult), r=[("qn", b), "cosT"], w=[("t1", b)])
                        S.dve(lambda e: e.tensor_tensor(out=t2[b][:, 0:n], in0=pr[b][:, 0:n], in1=sinT[:, l0:l0 + n],
                                                        op=ALU.mult), r=[("pr", b), "sinT"], w=[("t2", b)])
                        if dest_fn is None:
                            for hf in range(2):
                                S.dve(lambda e, hf=hf: e.tensor_tensor(
                                    out=gkT[hf * 64:(hf + 1) * 64, hf, s0:s0 + n],
                                    in0=t1[b][hf * 64:(hf + 1) * 64, 0:n], in1=t2[b][hf * 64:(hf + 1) * 64, 0:n],
                                    op=ALU.add), r=[("t1", b), ("t2", b), "gkz"], w=[("gqk", i)])
                        else:
                            S.dve(lambda e: e.tensor_tensor(out=dest_fn(s0, n), in0=t1[b][:, 0:n], in1=t2[b][:, 0:n],
                                                            op=ALU.add), r=[("t1", b), ("t2", b)], w=[("gqk", i)])

                    ng = len(gjobs)
                    for step in range(ng + 2):
                        if 0 <= step - 2 < ng:
                            g_p3(step - 2)
                        if 0 <= step - 1 < ng:
                            g_p2(step - 1)
                        if step < ng:
                            g_p1(step)
                    gc = ng
                    for t in range(TT):
                        b = gc % 2
                        gc += 1
                        for k in range(KC):
                            S.pe(lambda e, b=b, k=k, t=t: e.matmul(
                                pq[b][:, 0:128], lhsT=hT[:, k, t * 128:(t + 1) * 128], rhs=ws[1][:, k, 128:256],
                                start=(k == 0), stop=(k == KC - 1)), r=[("ws", 1)], w=[("pq", b)])
                        S.act(lambda e, b=b, t=t: e.activation(out=gvA[:, t, :, 0:64],
                                                               in_=pq[b][:, 0:128].rearrange("p (g d) -> p g d", g=2),
                                                               func=AF.Copy), r=[("pq", b)], w=[("gvA", t)])
                        S.act(lambda e, b=b, t=t: e.activation(out=gvB[:, t, :, 64:128],
                                                               in_=pq[b][:, 0:128].rearrange("p (g d) -> p g d", g=2),
                                                               func=AF.Copy), r=[("pq", b)], w=[("gvB", t)])
                    for a_, cbase in enumerate((GA, GB, GC)):
                        S.dma("pool", lambda e, a_=a_, cbase=cbase: e.dma_start(
                            out=Pw[:, :, a_ * 128:(a_ + 1) * 128], in_=w1v(l, cbase, 128)), w=[("ws", 0)])
                    pst = [palloc(ph, f"gpst{i}", [128, 512], F32) for i in range(3)]
                    pbc = palloc(ph, "gpbc", [128, 512], F32)
                    ppo = [pq[0], pq[1], pr[0], pr[1]]
                    pok = [("pq", 0), ("pq", 1), ("pr", 0), ("pr", 1)]
                    ptb = [alloc(ph, f"gptb{i}", [128, 512], BF16) for i in range(3)]
                    rd = [alloc(ph, f"grd{i}", [128, 256], F32) for i in range(2)]
                    bcs = [alloc(ph, f"gbcs{i}", [128, 256], F32) for i in range(2)]
                    onesf = alloc(ph, "gonesf", [128, 128], F32)
                    S.dve(lambda e: e.memset(onesf[:], 0.0), w=["onesf"])
                    S.dve(lambda e: e.memset(onesf[64:65, 0:64], 1.0), w=["onesf"])
                    S.dve(lambda e: e.memset(onesf[0:1, 64:128], 1.0), w=["onesf"])
                    for i_ in range(2):
                        S.dve(lambda e, i_=i_: e.memset(rd[i_][:], 0.0), w=[("rdA", i_), ("rdB", i_)])
                    its = []
                    grp = 0
                    for qt in qtiles:
                        ktiles = list(range(CT)) if qt < CT else list(range(TT))
                        for g in range(2):
                            for ki, ktile in enumerate(ktiles):
                                its.append((qt, g, ki, len(ktiles), ktile, grp))
                            grp += 1

                    def emit_S(i):
                        qt, g, ki, nk, ktile, gn = its[i]
                        b = i % 3
                        g0 = g * 64
                        tq = slice(qt * 128, (qt + 1) * 128)
                        S.pe(lambda e: e.matmul(
                            pst[b][:, :], lhsT=gkT[:, g, ktile * 128:(ktile + 1) * 128],
                            rhs=gqT[:, :, tq], start=True, stop=True), r=[("gqk", i_) for i_ in range(ng)],
                            w=[("pst", b)])

                    emit_S(0)
                    if len(its) > 1:
                        emit_S(1)
                    deferred = []
                    for i in range(len(its)):
                        qt, g, ki, nk, ktile, gn = its[i]
                        b = i % 3
                        if i + 2 < len(its):
                            emit_S(i + 2)
                        S.act(lambda e, b=b: e.activation(out=ptb[b][:], in_=pst[b][:], func=AF.Exp, scale=0.125),
                              r=[("pst", b)], w=[("ptb", b)])
                        pa_, pb_ = 2 * (gn % 2), 2 * (gn % 2) + 1
                        if ki == 0:
                            for d in [d for d in deferred if d[2] == gn % 2]:
                                d[1]()
                            deferred[:] = [d for d in deferred if d[2] != gn % 2]
                        pv = ptb[b][:].rearrange("p (j q) -> p j q", j=4)
                        S.pe(lambda e, pa_=pa_, pv=pv, ktile=ktile, g=g, ki=ki, nk=nk: e.matmul(
                            ppo[pa_][:, 0:256], lhsT=gvA[:, ktile, g, :], rhs=pv[:, 0::2, :],
                            start=(ki == 0), stop=(ki == nk - 1)), r=[("ptb", b), ("gvA", ktile), "gv1"], w=[pok[pa_]])
                        S.pe(lambda e, pb_=pb_, pv=pv, ktile=ktile, g=g, ki=ki, nk=nk: e.matmul(
                            ppo[pb_][:, 0:256], lhsT=gvB[:, ktile, g, :], rhs=pv[:, 1::2, :],
                            start=(ki == 0), stop=(ki == nk - 1)), r=[("ptb", b), ("gvB", ktile), "gv1"], w=[pok[pb_]])
                        for d in [d for d in deferred if d[0] <= i]:
                            d[1]()
                        deferred[:] = [d for d in deferred if d[0] > i]
                        if ki == nk - 1:
                            rb = gn % 2
                            tq = slice(qt * 128, (qt + 1) * 128)

                            def st1(rb=rb, pa_=pa_, pb_=pb_):
                                S.dve(lambda e: e.reciprocal(out=rd[rb][64:65, :], in_=ppo[pa_][64:65, 0:256]),
                                      r=[pok[pa_]], w=[("rdA", rb)])
                                S.dve(lambda e: e.reciprocal(out=rd[rb][0:1, :], in_=ppo[pb_][0:1, 0:256]),
                                      r=[pok[pb_]], w=[("rdB", rb)])

                            def st2(rb=rb):
                                S.pe(lambda e: e.matmul(pbc[:, 0:256], lhsT=onesf[:], rhs=rd[rb][:],
                                                        start=True, stop=True),
                                     r=[("rdA", rb), ("rdB", rb), "onesf"], w=["pbc"])

                            def st3(rb=rb, pa_=pa_, pb_=pb_, g=g, tq=tq):
                                S.dve(lambda e: e.tensor_copy(out=bcs[rb][:], in_=pbc[:, 0:256]),
                                      r=["pbc"], w=[("bcsA", rb), ("bcsB", rb)])
                                S.dve(lambda e: e.tensor_tensor(
                                    out=yT[0:64, 2, 2 * g:2 * g + 2, tq],
                                    in0=ppo[pa_][0:64, 0:256].rearrange("p (j q) -> p j q", j=2),
                                    in1=bcs[rb][0:64, :].rearrange("p (j q) -> p j q", j=2), op=ALU.mult),
                                    r=[pok[pa_], ("bcsA", rb)], w=[("yT2", 0, g, qt)])
                                S.dve(lambda e: e.tensor_tensor(
                                    out=yT[64:128, 2, 2 * g:2 * g + 2, tq],
                                    in0=ppo[pb_][64:128, 0:256].rearrange("p (j q) -> p j q", j=2),
                                    in1=bcs[rb][64:128, :].rearrange("p (j q) -> p j q", j=2), op=ALU.mult),
                                    r=[pok[pb_], ("bcsB", rb)], w=[("yT2", 1, g, qt)])

                            last_i = len(its) - 1
                            deferred.append((min(i + 1, last_i), st1, rb))
                            deferred.append((min(i + 4, last_i), st2, rb))
                            deferred.append((min(i + 6, last_i), st3, rb))
                    for d in deferred:
                        d[1]()
                    S.emit()
                if stop_after == "gqa":
                    return nc, yT
                with contextlib.ExitStack() as mg:
                    zT = alloc(mg, "zT", [128, KC, NT], BF16)
                    with contextlib.ExitStack() as ph:
                        S = Sched(nc)
                        wga = [alloc(ph, f"wga{i}", [128, KC, 3, 128], BF16) for i in range(2)]
                        wbb = [alloc(ph, f"wbb{i}", [128, 4, 3, 128], BF16) for i in range(2)]
                        pgt = [palloc(ph, f"pgt{i}", [128, 512], F32) for i in range(4)]
                        put = [palloc(ph, f"put{i}", [128, 512], F32) for i in range(4)]
                        sgm = [alloc(ph, f"sgm{i}", [128, 512], F32) for i in range(3)]
                        zac = [alloc(ph, f"zac{i}", [128, 512], F32) for i in range(2)]
                        ztm = [alloc(ph, f"ztm{i}", [128, 512], F32) for i in range(2)]
                        blocks = tok_blocks(c, with_ctx=not last)
                        pc = 0
                        zc = 0
                        Pg = Pw[:, :, 0:384].rearrange("p k (a n) -> p k a n", a=3)
                        for fc in range(KC):
                            wb = fc % 2
                            for a, cbase in enumerate((GA, GB, GC)):
                                if fc > 0:
                                    S.dma("pool", lambda e, wb=wb, a=a, cbase=cbase, fc=fc: e.dma_start(
                                        out=wga[wb][:, :, a, :], in_=w1v(l, cbase + fc * 128, 128)), w=[("wga", wb)])
                                S.dma("pool", lambda e, wb=wb, a=a, fc=fc: e.dma_start(
                                    out=wbb[wb][:, :, a, :],
                                    in_=wbr[l, a].rearrange("(k p) n -> p k n", p=128)[:, :, fc * 128:(fc + 1) * 128]),
                                    w=[("wbb", wb)])
                            if fc == 1:
                                S.dma("pool", lambda e: e.dma_start(
                                    out=Pw[:], in_=wout[l].rearrange("(k p) n -> p k n", p=128)[:, :, 0:512]), w=[("wga", 0)])
                            for (s0, n, isc) in blocks:
                                zb = zc % 2
                                zc += 1
                                for a in range(3):
                                    b = pc % 4
                                    pc += 1
                                    for k in range(KC):
                                        S.pe(lambda e, b=b, k=k, a=a, wb=wb, s0=s0, n=n, fc=fc: e.matmul(
                                            pgt[b][:, 0:n], lhsT=(Pg if fc == 0 else wga[wb])[:, k, a, :], rhs=hT[:, k, s0:s0 + n],
                                            start=(k == 0), stop=(k == KC - 1)), r=[("wga", wb)], w=[("pgt", b)])
                                    for k in range(4):
                                        S.pe(lambda e, b=b, k=k, a=a, wb=wb, s0=s0, n=n: e.matmul(
                                            put[b][:, 0:n], lhsT=wbb[wb][:, k, a, :], rhs=yT[:, a, k, s0:s0 + n],
                                            start=(k == 0), stop=(k == 3)), r=[("wbb", wb)], w=[("put", b)])
                                    S.act(lambda e, b=b, a=a, n=n: e.activation(out=sgm[a][:, 0:n], in_=pgt[b][:, 0:n],
                                                                                func=AF.Sigmoid),
                                          r=[("pgt", b)], w=[("sgm", a)])
                                    if a == 0:
                                        S.dve(lambda e, b=b, n=n, zb=zb: e.tensor_tensor(
                                            out=zac[zb][:, 0:n], in0=sgm[0][:, 0:n], in1=put[b][:, 0:n], op=ALU.mult),
                                            r=[("sgm", 0), ("put", b)], w=[("zac", zb)])
                                    else:
                                        S.dve(lambda e, b=b, a=a, n=n, zb=zb: e.tensor_tensor(
                                            out=ztm[zb][:, 0:n], in0=sgm[a][:, 0:n], in1=put[b][:, 0:n], op=ALU.mult),
                                            r=[("sgm", a), ("put", b)], w=[("ztm", zb)])
                                        if a == 1:
                                            S.dve(lambda e, n=n, zb=zb: e.tensor_tensor(
                                                out=zac[zb][:, 0:n], in0=zac[zb][:, 0:n], in1=ztm[zb][:, 0:n], op=ALU.add),
                                                r=[("zac", zb), ("ztm", zb)], w=[("zac", zb)])
                                        else:
                                            S.dve(lambda e, n=n, zb=zb, fc=fc, s0=s0: e.tensor_tensor(
                                                out=zT[:, fc, s0:s0 + n], in0=zac[zb][:, 0:n], in1=ztm[zb][:, 0:n],
                                                op=ALU.add), r=[("zac", zb), ("ztm", zb)], w=[("zT", fc, s0)])
                        S.emit()
                    with contextlib.ExitStack() as ph:
                        S = Sched(nc)
                        wo2 = alloc(ph, "wo2", [128, KC, max(D - 512, 512)], BF16)
                        for k2 in range(512, D, 512):
                            S.dma("pool", lambda e, k2=k2: e.dma_start(
                                out=wo2[:, :, k2 - 512:k2],
                                in_=wout[l].rearrange("(k p) n -> p k n", p=128)[:, :, k2:k2 + 512]), w=[("wo", k2)])

                        def wo_ap(k, hf):
                            return Pw[:, k, :] if hf == 0 else wo2[:, k, (hf - 1) * 512:hf * 512]
                        gg = alloc(ph, "gg", [128, 2, D], F32)
                        gpo = alloc(ph, "gpo", [128, D], F32)
                        S.dma("act", lambda e: e.dma_start(out=gpo[:], in_=gvec[l, 1:2, :].partition_broadcast(128)),
                              w=["gpo"])
                        for s in range(2):
                            S.dma("act", lambda e, s=s: e.dma_start(
                                out=gg[:, s, :], in_=modd[l, s:s + 1, 2 * D:3 * D].partition_broadcast(128)), w=[("gg", s)])
                            S.dve(lambda e, s=s: e.tensor_tensor(out=gg[:, s, :], in0=gg[:, s, :], in1=gpo[:], op=ALU.mult),
                                  r=[("gg", s), "gpo"], w=[("gg", s)])
                        py = [palloc(ph, f"py{i}", [128, 1024], F32) for i in range(2)]

                        def src_out(ti, t, b):
                            tk = slice(t * 128, (t + 1) * 128)
                            for hf in range(D // 512):
                                for k in range(KC):
                                    S.pe(lambda e, k=k, hf=hf: e.matmul(
                                        py[b][:, hf * 512:(hf + 1) * 512], lhsT=zT[:, k, tk],
                                        rhs=wo_ap(k, hf), start=(k == 0), stop=(k == KC - 1)),
                                        r=[("wo", hf * 512)], w=[("py", b)])
                            return py[b][:, 0:D], [("py", b)]

                        def store_x(S_, t, xt, key):
                            S_.dma("sp", lambda e: e.dma_start(out=xs[t * 128:(t + 1) * 128, :], in_=xt[:]), r=[key])

                        norm_pipeline(S, ph, qtiles, src_out, gg, l, 1, store_x, True, "o", from_inputs=(l == 0))
                        S.emit()
            if stop_after == "mix":
                return nc
            with contextlib.ExitStack() as mlp:
                acc = alloc(mlp, "acc", [128, TT, D], F32)
                blocks = tok_blocks(c, with_ctx=not last)
                with contextlib.ExitStack() as ph:
                    S = Sched(nc)
                    w1s = [alloc(ph, f"w1s{i}", [128, KC, 512], BF16) for i in range(2)]
                    w2s = [alloc(ph, f"w2s{i}", [128, 4, D], BF16) for i in range(2)]
                    pu = [palloc(ph, f"pu{i}", [128, 512], F32) for i in range(3)]
                    pd = [palloc(ph, f"pd{i}", [128, 512], F32) for i in range(4)]
                    rr = [alloc(ph, f"rr{i}", [128, 512], F32) for i in range(2)]
                    aT = [alloc(ph, f"aT{i}", [128, 4, 512], BF16) for i in range(2)]
                    uc = 0
                    dc = 0
                    ac = 0
                    do_mod = not last
                    if do_mod:
                        wmbb = [alloc(ph, f"wmbb{i}", [128, KC, 512], BF16) for i in range(3)]
                        pmm = palloc(ph, "pmm", [128, 512], F32)
                        bsk = [alloc(ph, f"bsk{i}", [2, 512], F32) for i in range(2)]
                        mdk = [alloc(ph, f"mdk{i}", [2, 512], F32) for i in range(2)]
                        nblk_ = 6 * D // 512
                        per_fb = (nblk_ + c.FB - 1) // c.FB
                        mcnt = 0

                    def mod_block(j):
                        b3, b2 = j % 3, j % 2
                        S.dma("pool", lambda e: e.dma_start(
                            out=wmbb[b3][:], in_=wmod[l + 1].rearrange("(k p) n -> p k n", p=128)[:, :, j * 512:(j + 1) * 512]),
                            w=[("wmbb", b3)])
                        for a_ in range(2):
                            S.dma("sp", lambda e, a_=a_: e.dma_start(out=bsk[b2][a_:a_ + 1, :],
                                                                     in_=bmod[l + 1:l + 2, j * 512:(j + 1) * 512]),
                                  w=[("bsk", b2, a_)])
                        for k in range(KC):
                            S.pe(lambda e, k=k: e.matmul(pmm[0:2, :], lhsT=scTb[:, k, :], rhs=wmbb[b3][:, k, :],
                                                         start=(k == 0), stop=(k == KC - 1)), r=[("wmbb", b3)], w=["pmm"])
                        S.dve(lambda e: e.tensor_tensor(out=mdk[b2][:], in0=pmm[0:2, :], in1=bsk[b2][:], op=ALU.add),
                              r=["pmm", ("bsk", b2, 0), ("bsk", b2, 1)], w=[("mdk", b2)])
                        S.dma("sp", lambda e: e.dma_start(out=modd[l + 1, :, j * 512:(j + 1) * 512], in_=mdk[b2][:]),
                              r=[("mdk", b2)])

                    mitems = []
                    mstate = [0]

                    def fb_pre(fb):
                        wb = fb % 2
                        S.dma("pool", lambda e: e.dma_start(
                            out=w1s[wb][:], in_=wm1[l].rearrange("(k p) n -> p k n", p=128)[:, :, fb * 512:(fb + 1) * 512]),
                            w=[("w1s", wb)])
                        S.dma("pool", lambda e: e.dma_start(
                            out=w2s[wb][:], in_=wm2[l, fb * 512:(fb + 1) * 512, :].rearrange("(k p) n -> p k n", p=128)),
                            w=[("w2s", wb)])
                        if do_mod:
                            for _ in range(per_fb):
                                if mstate[0] < nblk_:
                                    mod_block(mstate[0])
                                    mstate[0] += 1

                    for fb in range(c.FB):
                        wb = fb % 2
                        for (s0, n, isc) in blocks:
                            mitems.append((fb, wb, s0, n))

                    def m_up(i):
                        fb, wb, s0, n = mitems[i]
                        ab = i % 2
                        for f4 in range(4):
                            b = (i * 4 + f4) % 3
                            rb = (i * 4 + f4) % 2
                            for k in range(KC):
                                S.pe(lambda e, b=b, k=k, f4=f4: e.matmul(
                                    pu[b][:, 0:n], lhsT=w1s[wb][:, k, f4 * 128:(f4 + 1) * 128], rhs=hT[:, k, s0:s0 + n],
                                    start=(k == 0), stop=(k == KC - 1)), r=[("w1s", wb)], w=[("pu", b)])
                            S.act(lambda e, b=b, rb=rb: e.activation(out=rr[rb][:, 0:n], in_=pu[b][:, 0:n], func=AF.Relu),
                                  r=[("pu", b)], w=[("rr", rb)])
                            S.dve(lambda e, rb=rb, f4=f4: e.tensor_tensor(
                                out=aT[ab][:, f4, 0:n], in0=rr[rb][:, 0:n], in1=rr[rb][:, 0:n], op=ALU.mult),
                                r=[("rr", rb)], w=[("aT", ab)])

                    def m_down(i):
                        fb, wb, s0, n = mitems[i]
                        ab = i % 2
                        for ti in range(n // 128):
                            t = s0 // 128 + ti
                            for hf in range(D // 512):
                                b = dcn[0] % 4
                                dcn[0] += 1
                                for f4 in range(4):
                                    S.pe(lambda e, b=b, f4=f4, ti=ti, hf=hf: e.matmul(
                                        pd[b][:, :], lhsT=aT[ab][:, f4, ti * 128:(ti + 1) * 128],
                                        rhs=w2s[wb][:, f4, hf * 512:(hf + 1) * 512], start=(f4 == 0), stop=(f4 == 3)),
                                        r=[("aT", ab), ("w2s", wb)], w=[("pd", b)])
                                if fb == 0:
                                    S.act(lambda e, b=b, t=t, hf=hf: e.activation(
                                        out=acc[:, t, hf * 512:(hf + 1) * 512], in_=pd[b][:, :], func=AF.Copy),
                                        r=[("pd", b)], w=[("acc", t, hf)])
                                else:
                                    S.dve(lambda e, b=b, t=t, hf=hf: e.tensor_tensor(
                                        out=acc[:, t, hf * 512:(hf + 1) * 512], in0=acc[:, t, hf * 512:(hf + 1) * 512],
                                        in1=pd[b][:, :], op=ALU.add), r=[("pd", b), ("acc", t, hf)], w=[("acc", t, hf)])

                    dcn = [0]
                    nm = len(mitems)
                    for step in range(nm + 1):
                        if step < nm:
                            fbs = mitems[step][0]
                            if step == 0 or mitems[step - 1][0] != fbs:
                                fb_pre(fbs)
                            m_up(step)
                        if step - 1 >= 0:
                            m_down(step - 1)
                    S.emit()
                with contextlib.ExitStack() as ph:
                    S = Sched(nc)
                    gg = alloc(ph, "gg2", [128, 2, D], F32)
                    gpo = alloc(ph, "gpo2", [128, D], F32)
                    S.dma("act", lambda e: e.dma_start(out=gpo[:], in_=gvec[l, 3:4, :].partition_broadcast(128)), w=["gpo"])
                    for s in range(2):
                        S.dma("act", lambda e, s=s: e.dma_start(
                            out=gg[:, s, :], in_=modd[l, s:s + 1, 5 * D:6 * D].partition_broadcast(128)), w=[("gg", s)])
                        S.dve(lambda e, s=s: e.tensor_tensor(out=gg[:, s, :], in0=gg[:, s, :], in1=gpo[:], op=ALU.mult),
                              r=[("gg", s), "gpo"], w=[("gg", s)])
                    def src_acc(ti, t, b):
                        return acc[:, t, :], []

                    def store_x2(S_, t, xt, key):
                        if last:
                            lo = (t - CT) * 128
                            S_.dma("sp", lambda e: e.dma_start(out=out[lo:lo + 128, :], in_=xt[:]), r=[key])
                        else:
                            S_.dma("sp", lambda e: e.dma_start(out=xs[t * 128:(t + 1) * 128, :], in_=xt[:]), r=[key])

                    if not last:
                        S.dma("pool", lambda e: e.dma_start(out=Pw[:], in_=w1v(l + 1, RQD, 512)), w=["Pw"])
                        pp_params(S, ph, l + 1)
                    norm_pipeline(S, ph, qtiles, src_acc, gg, l + 1, 0, store_x2, not last, "m")
                    S.emit()
    return nc


def host_consts(c):
    NL = c.NL
    ident = np.eye(128, dtype=np.float32)
    perm = np.zeros((128, 128), np.float32)
    for m in range(128):
        perm[(m % 64) ^ 16 | (m & 64), m] = 1.0
    bones = np.zeros((128, 128), np.float32)
    bones[:64, :64] = 1.0
    bones[64:, 64:] = 1.0
    t = np.arange(NL)
    pos = np.stack([t // 64, t % 64], -1).astype(np.float32)
    nf = 16
    inv = (np.float32(10000.0) ** (-np.arange(nf, dtype=np.float32) / nf)).astype(np.float32)
    cosd = np.zeros((128, NL), np.float32)
    sind = np.zeros((128, NL), np.float32)
    for p in range(128):
        i = p % 64
        a, s, f = i // 32, (i // 16) % 2, i % 16
        ang = (pos[:, a] * inv[f]).astype(np.float32)
        cosd[p] = np.cos(ang)
        sind[p] = np.sin(ang) * (-1.0 if s == 0 else 1.0)
    j = np.arange(128)[:, None].astype(np.float32)
    i = np.arange(128)[None, :].astype(np.float32)
    ctab = np.stack([np.maximum(i - j, 0), (i >= j).astype(np.float32),
                     np.maximum(j - i, 0), (j > i).astype(np.float32)]).astype(np.float32)
    posq = np.zeros((128, 128), np.float32)
    posq[:64] = np.arange(128)[None, :] + 1.0
    posq[64:] = 128.0 - np.arange(128)[None, :]
    posk = np.zeros((128, 8), np.float32)
    posk[:, 0:4] = (127.0 - np.arange(128))[:, None]
    posk[:, 4:8] = np.arange(128)[:, None]
    return dict(ident=ident, perm=perm, bones=bones, cosd=cosd, sind=sind, ctab=ctab, posq=posq, posk=posk)


def host_layout(c, inp):
    L = c.L
    w_in = np.asarray(inp["w_in"])
    sp = np.cumsum([0, 256, 256, 512, 512, 512, 512, 512, 512, 128, 128, c.D, c.D, c.D])
    cols = []
    rq0 = sp[0]
    for h in range(4):
        cols += list(range(rq0 + h * 64, rq0 + (h + 1) * 64)) * 2
    cols += list(range(sp[1], sp[2]))
    cols += list(range(sp[2], sp[3]))
    cols += list(range(sp[3], sp[4]))
    cols += list(range(sp[4], sp[5]))
    cols += list(range(sp[5], sp[6]))
    cols += list(range(sp[6], sp[7]))
    gq0 = sp[7]
    for cch in range(4):
        cols += list(range(gq0 + cch * 64, gq0 + (cch + 1) * 64))
        cols += list(range(gq0 + (4 + cch) * 64, gq0 + (5 + cch) * 64))
    cols += list(range(sp[8], sp[9]))
    cols += list(range(sp[9], sp[10]))
    assert len(cols) == GA
    w1 = np.zeros((L, c.D, W1C), np.float32)
    w1[:, :, :GA] = w_in[:, :, cols]
    w1[:, :, GA:GA + c.D] = w_in[:, :, sp[10]:sp[11]]
    w1[:, :, GB:GB + c.D] = w_in[:, :, sp[11]:sp[12]]
    w1[:, :, GC:GC + c.D] = w_in[:, :, sp[12]:sp[13]]
    wbr = np.stack([inp["w_br_ret"], inp["w_br_na"], inp["w_br_gqa"]], 1).astype(np.float32)
    gvec = np.stack([inp["g_pre_mix"], inp["g_post_mix"], inp["g_pre_mlp"], inp["g_post_mlp"]], 1).astype(np.float32)
    rdl = np.asarray(inp["ret_decay_logit"], np.float32).reshape(L, 8)
    rb = np.asarray(inp["na_rel_bias"], np.float32)
    col = np.arange(64)
    cs = np.clip(col - 8, 0, 64 - 16)
    inw = (col[None, :] >= cs[:, None]) & (col[None, :] < cs[:, None] + 16)
    dcol = np.clip(col[None, :] - col[:, None], -15, 15) + 15
    rbp = np.concatenate([rb, np.full((L, 8, 15, 1), NEG, np.float32)], -1)
    idx = np.where(inw, dcol, 31)
    g = rbp[:, :, :, idx]
    g = np.transpose(g, (0, 4, 1, 2, 3))
    nab = np.full((L, 64, 8, 17, 64), NEG, np.float32)
    nab[:, :, :, 1:16, :] = g
    nab = nab.reshape(L, 64, 8 * 17 * 64)
    gqn = np.stack([np.concatenate([inp["gqa_q_norm"]] * 2, -1), np.concatenate([inp["gqa_k_norm"]] * 2, -1)], 1)
    shared = dict(w1=w1, wbr=wbr, wout=np.asarray(inp["w_out"], np.float32),
                  wm1=np.asarray(inp["w_mlp_in"], np.float32), wm2=np.asarray(inp["w_mlp_out"], np.float32),
                  wmod=np.asarray(inp["w_mod"], np.float32), bmod=np.asarray(inp["b_mod"], np.float32),
                  gvec=gvec, rdl=rdl, nab=nab, gqn=gqn.astype(np.float32))
    shared.update(host_consts(c))
    return shared


_CACHE = {}


def kernel(**inp):
    c = Cfg()
    shared = host_layout(c, inp)
    x = np.asarray(inp["x"], np.float32)
    ctx = np.asarray(inp["ctx"], np.float32)
    cv = np.asarray(inp["c"], np.float32)
    cctx = np.asarray(inp["c_ctx"], np.float32)
    B = x.shape[0]
    in_maps = []
    for b in range(B):
        m = dict(shared)
        m["x_in"] = np.ascontiguousarray(x[b])
        m["ctx_in"] = np.ascontiguousarray(ctx[b])
        m["cc"] = np.stack([cv[b], cctx], 0)
        in_maps.append(m)
    nc = build(c)
    res = run_bass_kernel_spmd(nc, in_maps, core_ids=list(range(B)))
    return np.stack([r["out"] for r in res.results], 0).astype(np.float32)
```

```python
import contextlib
import numpy as np
import concourse.bass as bass
import concourse.mybir as mybir
from concourse.bass_utils import run_bass_kernel_spmd

F32 = mybir.dt.float32
BF16 = mybir.dt.bfloat16
AF = mybir.ActivationFunctionType
ALU = mybir.AluOpType

COMPUTE = ("pe", "act", "dve", "pool")
SAME_ENG_SYNC = True
EPS = 1e-6
NEG = -1e30

RQD, RK, RV, RG, NQ, NK, NV, GQ, GK, GV, GA, GB, GC, W1C = (
    0, 512, 768, 1280, 1792, 2304, 2816, 3328, 3840, 3968, 4096, 5120, 6144, 7168)


class Sched:
    PH = 0

    def __init__(self, nc):
        self.nc = nc
        self.ops = []
        self.last_w = {}
        self.rd_c = {}
        self.rd_d = {}

    def add(self, eng, fn, reads=(), writes=(), dma=False):
        oid = len(self.ops)
        deps = set()
        for k in reads:
            w = self.last_w.get(k)
            if w is not None:
                deps.add(w)
        for k in writes:
            w = self.last_w.get(k)
            if w is not None:
                deps.add(w)
            for r in self.rd_c.get(k, {}).values():
                deps.add(r)
            for r in self.rd_d.get(k, ()):
                deps.add(r)
        for k in reads:
            if dma:
                self.rd_d.setdefault(k, []).append(oid)
            else:
                self.rd_c.setdefault(k, {})[eng] = oid
        for k in writes:
            self.last_w[k] = oid
            self.rd_c[k] = {}
            self.rd_d[k] = []
        deps.discard(oid)
        self.ops.append(dict(id=oid, eng=eng, fn=fn, deps=deps, dma=dma, sig=None))
        return oid

    def pe(self, fn, r=(), w=()):
        return self.add("pe", fn, r, w)

    def act(self, fn, r=(), w=()):
        return self.add("act", fn, r, w)

    def dve(self, fn, r=(), w=()):
        return self.add("dve", fn, r, w)

    def dma(self, q, fn, r=(), w=()):
        return self.add(q, fn, r, w, dma=True)

    def emit(self):
        nc = self.nc
        ops = self.ops
        if not ops:
            return
        need_sig = [False] * len(ops)
        for op in ops:
            if op["dma"]:
                need_sig[op["id"]] = True
            for d in op["deps"]:
                p = ops[d]
                if p["dma"]:
                    continue
                if p["eng"] == op["eng"] and not op["dma"]:
                    if p["eng"] == "pe" or not SAME_ENG_SYNC:
                        continue
                need_sig[d] = True
        engs = {}
        for op in ops:
            engs.setdefault(op["eng"], []).append(op)
        for e, lst in engs.items():
            for op in reversed(lst):
                if not op["dma"]:
                    need_sig[op["id"]] = True
                    break
        stack = nc.cleanup_on_exit()
        sems = {}
        cnt = {e: 0 for e in COMPUTE}
        NSLOT = 6
        slot_rr = {}
        slot_uses = {}
        for op in ops:
            if op["dma"]:
                q = op["eng"]
                i = slot_rr.get(q, 0)
                slot_rr[q] = (i + 1) % NSLOT
                nm = f"d_{q}_{i}"
                u = slot_uses.get(nm, 0) + 1
                slot_uses[nm] = u
                op["sig"] = (nm, 16 * u)
                op["slot_prev"] = (nm, 16 * (u - 1))
            elif need_sig[op["id"]]:
                e = op["eng"]
                cnt[e] += 1
                op["sig"] = (f"c_{e}", cnt[e])
        last_sig = {}
        for op in ops:
            if op["sig"] is not None:
                nm = op["sig"][0]
                last_sig[nm] = max(last_sig.get(nm, 0), op["sig"][1])

        with stack:
          for nm in last_sig:
            Sched.PH += 1
            sems[nm] = nc.alloc_semaphore(name=f"{nm}_{Sched.PH}")
          with nc.Block() as block:
              def run_engine(e, eobj):
                  waited = {}

                  def wait(nm, val):
                      if val <= 0 or waited.get(nm, 0) >= val:
                          return
                      eobj.wait_ge(sems[nm], val)
                      waited[nm] = val

                  for op in engs.get(e, []):
                      need = {}
                      for d in op["deps"]:
                          p = ops[d]
                          if (not p["dma"]) and p["eng"] == e and not op["dma"]:
                              if e == "pe" or not SAME_ENG_SYNC:
                                  continue
                          nm, val = p["sig"]
                          need[nm] = max(need.get(nm, 0), val)
                      if op["dma"]:
                          nm, val = op["slot_prev"]
                          need[nm] = max(need.get(nm, 0), val)
                      for nm, val in need.items():
                          wait(nm, val)
                      ins = op["fn"](eobj)
                      if op["sig"] is not None:
                          ins.then_inc(sems[op["sig"][0]], 16 if op["dma"] else 1)
                  if e == "sp":
                      for nm, val in last_sig.items():
                          wait(nm, val)

              if "pe" in engs:
                  @block.tensor
                  def _(eng):
                      run_engine("pe", eng)
              if "act" in engs:
                  @block.scalar
                  def _(eng):
                      run_engine("act", eng)
              if "dve" in engs:
                  @block.vector
                  def _(eng):
                      run_engine("dve", eng)
              if "pool" in engs:
                  @block.gpsimd
                  def _(eng):
                      run_engine("pool", eng)

              @block.sync
              def _(eng):
                  run_engine("sp", eng)


class Cfg:
    def __init__(self, D=1024, R=32, CTX=256, DFF=4096, L=2):
        self.D, self.R, self.CTX, self.DFF, self.L = D, R, CTX, DFF, L
        self.KC = D // 128
        self.NL = R * 64
        self.NT = CTX + self.NL
        self.TT = self.NT // 128
        self.CT = CTX // 128
        self.LT = self.NL // 128
        self.FB = DFF // 512


def tok_blocks(c, with_ctx=True):
    out = []
    if with_ctx:
        s = 0
        while s < c.CTX:
            n = min(512, c.CTX - s)
            out.append((s, n, True))
            s += n
    s = c.CTX
    while s < c.NT:
        n = min(512, c.NT - s)
        out.append((s, n, False))
        s += n
    return out


def build(c, stop_after=None):
    nc = bass.Bass("TRN2", target_bir_lowering=False)
    D, KC, NL, NT, TT, CT, LT, CTX, L, DFF = c.D, c.KC, c.NL, c.NT, c.TT, c.CT, c.LT, c.CTX, c.L, c.DFF

    def din(name, shape):
        return nc.dram_tensor(name, list(shape), F32, kind="ExternalInput").ap()

    x_in = din("x_in", [NL, D])
    ctx_in = din("ctx_in", [CTX, D])
    cc = din("cc", [2, D])
    w1 = din("w1", [L, D, W1C])
    wbr = din("wbr", [L, 3, 512, D])
    wout = din("wout", [L, D, D])
    wm1 = din("wm1", [L, D, DFF])
    wm2 = din("wm2", [L, DFF, D])
    wmod = din("wmod", [L, D, 6 * D])
    bmod = din("bmod", [L, 6 * D])
    gvec = din("gvec", [L, 4, D])
    rdl = din("rdl", [L, 8])
    nab = din("nab", [L, 64, 8 * 17 * 64])
    gqn = din("gqn", [L, 2, 128])
    ident = din("ident", [128, 128])
    perm = din("perm", [128, 128])
    bones = din("bones", [128, 128])
    cosd = din("cosd", [128, NL])
    sind = din("sind", [128, NL])
    ctab = din("ctab", [4, 128, 128])
    posq = din("posq", [128, 128])
    posk = din("posk", [128, 8])
    out = nc.dram_tensor("out", [NL, D], F32, kind="ExternalOutput").ap()
    xs = nc.dram_tensor("xs", [NT, D], F32).ap()
    modd = nc.dram_tensor("modd", [L, 2, 6 * D], F32).ap()

    def w1v(l, c0, n):
        return w1[l].rearrange("(k p) n -> p k n", p=128)[:, :, c0:c0 + n]

    glob = contextlib.ExitStack()

    uid = [0]

    def alloc(stack, name, shape, dt):
        uid[0] += 1
        return stack.enter_context(nc.sbuf_tensor(f"{name}_{uid[0]}", list(shape), dt))

    def palloc(stack, name, shape, dt):
        uid[0] += 1
        return stack.enter_context(nc.psum_tensor(f"{name}_{uid[0]}", list(shape), dt))

    with glob:
        identb = alloc(glob, "identb", [128, 128], BF16)
        permb = alloc(glob, "permb", [128, 128], BF16)
        bonesb = alloc(glob, "bonesb", [128, 128], BF16)
        hT = alloc(glob, "hT", [128, KC, NT], BF16)
        ppar = alloc(glob, "ppar", [128, L * 2 * 2 * 2 * KC], F32)

        def pp(l, s, j, a):
            o = (((l * 2 + s) * 2 + j) * 2 + a) * KC
            return ppar[:, o:o + KC]

        scTb = alloc(glob, "scTb", [128, KC, 2], BF16)
        Pw = alloc(glob, "Pw", [128, KC, 512], BF16)

        def pp_params(S, ph, l):
            gpp = alloc(ph, f"gpp{l}", [128, 2, KC], F32)
            S.dma("act", lambda e: e.dma_start(
                out=gpp[:, 0, :], in_=gvec[l, 0].rearrange("(k p) -> p k", p=128), allow_slow_non_contiguous=True),
                w=[("gpp", l, 0)])
            S.dma("act", lambda e: e.dma_start(
                out=gpp[:, 1, :], in_=gvec[l, 2].rearrange("(k p) -> p k", p=128), allow_slow_non_contiguous=True),
                w=[("gpp", l, 1)])
            for s_ in range(2):
                for j in range(2):
                    sct = alloc(ph, f"sct{l}{s_}{j}", [128, KC], F32)
                    S.dma("act", lambda e, s_=s_, j=j, sct=sct: e.dma_start(
                        out=sct[:], in_=modd[l, s_, (3 * j + 1) * D:(3 * j + 2) * D].rearrange("(k p) -> p k", p=128),
                        allow_slow_non_contiguous=True), r=[("modd", l)], w=[("sct", l, s_, j)])
                    S.dma("act", lambda e, s_=s_, j=j: e.dma_start(
                        out=pp(l, s_, j, 1), in_=modd[l, s_, (3 * j) * D:(3 * j + 1) * D].rearrange("(k p) -> p k", p=128),
                        allow_slow_non_contiguous=True), r=[("modd", l)], w=[("ppall", l, s_, j, 1)])
                    S.dve(lambda e, s_=s_, j=j, sct=sct: e.scalar_tensor_tensor(
                        out=pp(l, s_, j, 0), in0=sct[:], scalar=1.0, in1=gpp[:, j, :], op0=ALU.add, op1=ALU.mult),
                        r=[("sct", l, s_, j), ("gpp", l, j)], w=[("ppall", l, s_, j, 0)])

        with contextlib.ExitStack() as ph:
            S = Sched(nc)
            S.dma("pool", lambda e: e.dma_start(out=identb[:], in_=ident), w=["identb"])
            S.dma("pool", lambda e: e.dma_start(out=permb[:], in_=perm), w=["permb"])
            S.dma("pool", lambda e: e.dma_start(out=bonesb[:], in_=bones), w=["bonesb"])
            ccT = alloc(ph, "ccT", [128, KC, 2], F32)
            scT = alloc(ph, "scT", [128, KC, 2], F32)
            for a in range(2):
                S.dma("sp", lambda e, a=a: e.dma_start(out=ccT[:, :, a], in_=cc[a].rearrange("(k p) -> p k", p=128),
                                                       allow_slow_non_contiguous=True), w=[("ccT", a)])
            S.act(lambda e: e.activation(out=scT[:], in_=ccT[:], func=AF.Silu), r=[("ccT", 0), ("ccT", 1)], w=["scT"])
            modsb = alloc(ph, "modsb", [2, 6 * D], F32)
            bsb = alloc(ph, "bsb", [2, 6 * D], F32)
            wmb = [alloc(ph, f"wmb{i}", [128, KC, 512], F32) for i in range(4)]
            pm = [palloc(ph, f"pm{i}", [128, 512], F32) for i in range(2)]
            nblk = 6 * D // 512
            it = 0
            S.act(lambda e: e.activation(out=scTb[:], in_=scT[:], func=AF.Copy), r=["scT"], w=["scTb"])
            for l in range(1):
                for a in range(2):
                    S.dma("sp", lambda e, l=l, a=a: e.dma_start(out=bsb[a:a + 1, :], in_=bmod[l:l + 1, :]),
                          w=[("bsb", a)])
                for j in range(nblk):
                    b = it % 4
                    it += 1
                    S.dma("sp", lambda e, l=l, j=j, b=b: e.dma_start(
                        out=wmb[b][:], in_=wmod[l].rearrange("(k p) n -> p k n", p=128)[:, :, j * 512:(j + 1) * 512]),
                        w=[("wmb", b)])
                    for k in range(KC):
                        S.pe(lambda e, b=b, k=k: e.matmul(pm[b % 2][0:2, :], lhsT=scT[:, k, :], rhs=wmb[b][:, k, :],
                                                           start=(k == 0), stop=(k == KC - 1)),
                             r=["scT", ("wmb", b)], w=[("pm", b % 2)])
                    S.dve(lambda e, b=b, j=j: e.tensor_tensor(out=modsb[:, j * 512:(j + 1) * 512], in0=pm[b % 2][0:2, :],
                                                              in1=bsb[:, j * 512:(j + 1) * 512], op=ALU.add),
                          r=[("pm", b % 2), ("bsb", 0), ("bsb", 1)], w=["modsb"])
                S.dma("sp", lambda e, l=l: e.dma_start(out=modd[l], in_=modsb[:]), r=["modsb"], w=[("modd", l)])
            pp_params(S, ph, 0)
            S.emit()
        if stop_after == "mod":
            return nc

        def make_hT_ops(S, t, xt, l, j, tmp, pT):
            s = 0 if t < CT else 1
            s = 1 - s
            xk, junk, ssv, rst, xn = tmp
            q = id(xn)
            S.act(lambda e: e.activation(out=junk[:], in_=xt[:], func=AF.Square, accum_out=ssv[:]),
                  r=[xk], w=[("junk", q), ("ssv", q)])
            S.act(lambda e: e.activation(out=rst[:], in_=ssv[:], func=AF.Sqrt, scale=1.0 / D, bias=EPS),
                  r=[("ssv", q)], w=[("rst", q)])
            S.dve(lambda e: e.reciprocal(out=rst[:], in_=rst[:]), r=[("rst", q)], w=[("rst", q)])
            S.act(lambda e: e.activation(out=xn[:], in_=xt[:], func=AF.Copy, scale=rst[:]),
                  r=[xk, ("rst", q)], w=[("xn", q)])
            for k in range(KC):
                S.pe(lambda e, k=k: e.transpose(out=pT[:, k * 128:(k + 1) * 128], in_=xn[:, k * 128:(k + 1) * 128],
                                                identity=identb[:]), r=[("xn", q), "identb"], w=[("pT", id(pT))])
            for k in range(KC):
                S.dve(lambda e, k=k: e.tensor_scalar(
                    out=hT[:, k, t * 128:(t + 1) * 128], in0=pT[:, k * 128:(k + 1) * 128],
                    scalar1=pp(l, s, j, 0)[:, k:k + 1], scalar2=pp(l, s, j, 1)[:, k:k + 1],
                    op0=ALU.mult, op1=ALU.add), r=[("pT", id(pT))], w=[("hT", t)])

        def htmp(ph, S, n=""):
            junk = alloc(ph, "junk" + n, [128, D], F32)
            ssv = alloc(ph, "ssv" + n, [128, 1], F32)
            rst = alloc(ph, "rst" + n, [128, 1], F32)
            xn = alloc(ph, "xn" + n, [128, D], BF16)
            return junk, ssv, rst, xn

        def x_tile_src(t, from_inputs):
            if not from_inputs:
                return xs[t * 128:(t + 1) * 128, :]
            if t < CT:
                return ctx_in[t * 128:(t + 1) * 128, :]
            return x_in[(t - CT) * 128:(t - CT + 1) * 128, :]

        def norm_pipeline(S, ph, tiles, src, gg, l_next, j_next, store, do_hT, tag, from_inputs=False):
            xts = [alloc(ph, f"{tag}xt{i}", [128, D], F32) for i in range(2)]
            tts = [alloc(ph, f"{tag}tt{i}", [128, D], F32) for i in range(2)]
            jk = [alloc(ph, f"{tag}jk{i}", [128, D], F32) for i in range(2)]
            xns = [alloc(ph, f"{tag}xn{i}", [128, D], BF16) for i in range(2)]
            ssy = [alloc(ph, f"{tag}ssy{i}", [128, 1], F32) for i in range(2)]
            ssv = [alloc(ph, f"{tag}ssv{i}", [128, 1], F32) for i in range(2)]
            pTs = [[palloc(ph, f"{tag}pT{i}{par}", [128, 1024], BF16) for par in range(2)] for i in range(2)]
            srcs = {}
            act_ks = [k for k in range(KC) if k % 3 == 1][:3]
            kmap = {}
            for k in range(KC):
                if k in act_ks:
                    kmap[k] = (1, act_ks.index(k))
                else:
                    kmap[k] = (0, len([x for x in range(k) if x not in act_ks]))

            def s1a(ti):
                t = tiles[ti]
                b = ti % 2
                if src is not None:
                    srcs[ti] = src(ti, t, b)

            def s1(ti):
                t = tiles[ti]
                b = ti % 2
                S.dma("sp", lambda e: e.dma_start(out=xts[b][:], in_=x_tile_src(t, from_inputs)), w=[("xt", b)])
                if src is not None:
                    yap, ykeys = srcs[ti]
                    S.act(lambda e: e.activation(out=jk[0][:], in_=yap, func=AF.Square, accum_out=ssy[b][:]),
                          r=ykeys, w=[("jk", 0), ("ssy", b)])
                    S.act(lambda e: e.activation(out=ssy[b][:], in_=ssy[b][:], func=AF.Sqrt, scale=1.0 / D, bias=EPS),
                          r=[("ssy", b)], w=[("ssy", b)])

            def s2(ti):
                t = tiles[ti]
                b = ti % 2
                sidx = 1 if t < CT else 0
                if src is not None:
                    yap, ykeys = srcs[ti]
                    S.dve(lambda e: e.reciprocal(out=ssy[b][:], in_=ssy[b][:]), r=[("ssy", b)], w=[("ssy", b)])
                    S.dve(lambda e: e.scalar_tensor_tensor(out=tts[b][:], in0=yap, scalar=ssy[b][:, 0:1], in1=gg[:, sidx, :],
                                                           op0=ALU.mult, op1=ALU.mult),
                          r=list(ykeys) + [("ssy", b), ("gg", sidx)], w=[("tt", b)])
                    S.dve(lambda e: e.tensor_tensor(out=xts[b][:], in0=xts[b][:], in1=tts[b][:], op=ALU.add),
                          r=[("xt", b), ("tt", b)], w=[("xt", b)])
                    store(S, t, xts[b], ("xt", b))
                if do_hT:
                    S.act(lambda e: e.activation(out=jk[1][:], in_=xts[b][:], func=AF.Square, accum_out=ssv[b][:]),
                          r=[("xt", b)], w=[("jk", 1), ("ssv", b)])
                    S.act(lambda e: e.activation(out=ssv[b][:], in_=ssv[b][:], func=AF.Sqrt, scale=1.0 / D, bias=EPS),
                          r=[("ssv", b)], w=[("ssv", b)])

            def s3(ti):
                b = ti % 2
                if not do_hT:
                    return
                S.dve(lambda e: e.reciprocal(out=ssv[b][:], in_=ssv[b][:]), r=[("ssv", b)], w=[("ssv", b)])
                S.act(lambda e: e.activation(out=xns[b][:], in_=xts[b][:], func=AF.Copy, scale=ssv[b][:]),
                      r=[("xt", b), ("ssv", b)], w=[("xn", b)])
                for k in range(KC):
                    par, pos = kmap[k]
                    S.pe(lambda e, k=k, par=par, pos=pos: e.transpose(
                        out=pTs[b][par][:, pos * 128:(pos + 1) * 128],
                        in_=xns[b][:, k * 128:(k + 1) * 128], identity=identb[:]),
                        r=[("xn", b)], w=[("pT", b, par)])

            def s4(ti):
                t = tiles[ti]
                b = ti % 2
                if not do_hT:
                    return
                sidx = 1 if t < CT else 0
                for k in range(KC):
                    par, pos = kmap[k]
                    if par == 0:
                        S.dve(lambda e, k=k, pos=pos: e.tensor_scalar(
                            out=hT[:, k, t * 128:(t + 1) * 128], in0=pTs[b][0][:, pos * 128:(pos + 1) * 128],
                            scalar1=pp(l_next, sidx, j_next, 0)[:, k:k + 1], scalar2=pp(l_next, sidx, j_next, 1)[:, k:k + 1],
                            op0=ALU.mult, op1=ALU.add),
                            r=[("pT", b, 0), ("ppall", l_next, sidx, j_next, 0), ("ppall", l_next, sidx, j_next, 1)],
                            w=[("hT", t, k)])
                    else:
                        S.act(lambda e, k=k, pos=pos: e.activation(
                            out=hT[:, k, t * 128:(t + 1) * 128], in_=pTs[b][1][:, pos * 128:(pos + 1) * 128],
                            func=AF.Identity,
                            scale=pp(l_next, sidx, j_next, 0)[:, k:k + 1], bias=pp(l_next, sidx, j_next, 1)[:, k:k + 1]),
                            r=[("pT", b, 1), ("ppall", l_next, sidx, j_next, 0), ("ppall", l_next, sidx, j_next, 1)],
                            w=[("hT", t, k)])

            n = len(tiles)
            stages = (s1, s2, s3, s4)
            for step in range(n + 3):
                if step < n:
                    s1a(step)
                for si in (3, 2, 1, 0):
                    i = step - si
                    if 0 <= i < n:
                        stages[si](i)

        with contextlib.ExitStack() as ph:
            S = Sched(nc)
            S.dma("pool", lambda e: e.dma_start(out=Pw[:], in_=w1v(0, RQD, 512)), w=["Pw"])
            norm_pipeline(S, ph, list(range(TT)), None, None, 0, 0, None, True, "i", from_inputs=True)
            S.emit()

        for l in range(L):
            last = (l == L - 1)
            qtiles = list(range(CT, TT)) if last else list(range(TT))
            with contextlib.ExitStack() as mix:
                yT = alloc(mix, "yT", [128, 3, 4, NT], BF16)
                with contextlib.ExitStack() as ret:
                    rqT = alloc(ret, "rqT", [128, 4, NT], BF16)
                    rkT = alloc(ret, "rkT", [128, 2, NT], BF16)
                    rv = alloc(ret, "rv", [128, TT, 512], BF16)
                    DT = alloc(ret, "DT", [128, 4, 128], F32)
                    decq = alloc(ret, "decq", [128, 4, 128], F32)
                    deckf = alloc(ret, "deckf", [128, 4, 2, 64], F32)
                    decsf = alloc(ret, "decsf", [128, 512], F32)
                    with contextlib.ExitStack() as ph:
                        S = Sched(nc)
                        cosT = alloc(ph, "cosT", [128, NL], BF16)
                        sinT = alloc(ph, "sinT", [128, NL], BF16)
                        S.dma("pool", lambda e: e.dma_start(out=cosT[:], in_=cosd), w=["cosT"])
                        S.dma("pool", lambda e: e.dma_start(out=sinT[:], in_=sind), w=["sinT"])
                        ws = [Pw] + [alloc(ph, f"ws{i}", [128, KC, 512], BF16) for i in range(1, 3)]
                        pq = [palloc(ph, f"pq{i}", [128, 512], F32) for i in range(3)]
                        pr = [palloc(ph, f"pr{i}", [128, 512], F32) for i in range(2)]
                        qb = [alloc(ph, f"qb{i}", [128, 512], BF16) for i in range(4)]
                        t1 = [alloc(ph, f"t1{i}", [128, 512], BF16) for i in range(2)]
                        t2 = [alloc(ph, f"t2{i}", [128, 512], BF16) for i in range(2)]
                        cnt = [0, 0, 0]

                        rjobs = []

                        def rope_proj(S, wsb, col, dest_fn, dk):
                            for (s0, n, isc) in tok_blocks(c):
                                rjobs.append((wsb, col, dest_fn, dk, s0, n, isc))

                        def r_p1(i):
                            wsb, col, dest_fn, dk, s0, n, isc = rjobs[i]
                            b = i % 3
                            for k in range(KC):
                                S.pe(lambda e, k=k: e.matmul(
                                    pq[b][:, 0:n], lhsT=ws[wsb][:, k, col:col + 128], rhs=hT[:, k, s0:s0 + n],
                                    start=(k == 0), stop=(k == KC - 1)), r=[("ws", wsb)], w=[("pq", b)])
                            if isc:
                                S.act(lambda e: e.activation(out=dest_fn(s0, n), in_=pq[b][:, 0:n], func=AF.Copy),
                                      r=[("pq", b)], w=[(dk, i)])
                            else:
                                b4 = i % 4
                                S.act(lambda e: e.activation(out=qb[b4][:, 0:n], in_=pq[b][:, 0:n], func=AF.Copy),
                                      r=[("pq", b)], w=[("qb", b4)])

                        def r_pB(i):
                            wsb, col, dest_fn, dk, s0, n, isc = rjobs[i]
                            if isc:
                                return
                            b2, b4 = i % 2, i % 4
                            S.pe(lambda e: e.matmul(pr[b2][:, 0:n], lhsT=permb[:], rhs=qb[b4][:, 0:n], start=True, stop=True),
                                 r=[("qb", b4), "permb"], w=[("pr", b2)])

                        def r_p2(i):
                            wsb, col, dest_fn, dk, s0, n, isc = rjobs[i]
                            if isc:
                                return
                            b2, b4 = i % 2, i % 4
                            l0 = s0 - CTX
                            S.dve(lambda e: e.tensor_tensor(out=t1[b2][:, 0:n], in0=qb[b4][:, 0:n], in1=cosT[:, l0:l0 + n],
                                                            op=ALU.mult), r=[("qb", b4), "cosT"], w=[("t1", b2)])
                            S.dve(lambda e: e.tensor_tensor(out=t2[b2][:, 0:n], in0=pr[b2][:, 0:n], in1=sinT[:, l0:l0 + n],
                                                            op=ALU.mult), r=[("pr", b2), "sinT"], w=[("t2", b2)])
                            S.dve(lambda e: e.tensor_tensor(out=dest_fn(s0, n), in0=t1[b2][:, 0:n], in1=t2[b2][:, 0:n],
                                                            op=ALU.add), r=[("t1", b2), ("t2", b2)], w=[(dk, i)])

                        S.dma("pool", lambda e: e.dma_start(out=ws[1][:, :, 0:256], in_=w1v(l, RK, 256)), w=[("ws", 1)])
                        S.dma("pool", lambda e: e.dma_start(out=ws[2][:], in_=w1v(l, RV, 512)), w=[("ws", 2)])
                        for h in range(4):
                            rope_proj(S, 0, h * 128, lambda s0, n, h=h: rqT[:, h, s0:s0 + n], "rqT")
                        for j in range(2):
                            rope_proj(S, 1, j * 128, lambda s0, n, j=j: rkT[:, j, s0:s0 + n], "rkT")
                        for step in range(len(rjobs) + 3):
                            if 0 <= step - 3 < len(rjobs):
                                r_p2(step - 3)
                            if 0 <= step - 2 < len(rjobs):
                                r_pB(step - 2)
                            if step < len(rjobs):
                                r_p1(step)
                        lgr = alloc(ph, "lgr", [128, 8], F32)
                        lg = alloc(ph, "lg", [128, 8], F32)
                        lgq = alloc(ph, "lgq", [128, 4], F32)
                        decs = alloc(ph, "decs", [128, 4], F32)
                        deck = alloc(ph, "deck", [128, 8], F32)
                        poskt = alloc(ph, "poskt", [128, 8], F32)
                        posqt = alloc(ph, "posqt", [128, 128], F32)
                        ctt = alloc(ph, "ctt", [128, 4, 128], F32)
                        dtmp = alloc(ph, "dtmp", [128, 2, 128], F32)
                        S.dma("sp", lambda e: e.dma_start(out=lgr[:], in_=rdl[l:l + 1, :].partition_broadcast(128)),
                              w=["lgr"])
                        S.dma("sp", lambda e: e.dma_start(out=poskt[:], in_=posk), w=["poskt"])
                        S.dma("sp", lambda e: e.dma_start(out=posqt[:], in_=posq), w=["posqt"])
                        S.dma("sp", lambda e: e.dma_start(out=ctt[:], in_=ctab.rearrange("a p n -> p a n")), w=["ctt"])
                        S.act(lambda e: e.activation(out=lg[:], in_=lgr[:], func=AF.Exp, scale=-1.0), r=["lgr"], w=["lg"])
                        S.act(lambda e: e.activation(out=lg[:], in_=lg[:], func=AF.Ln, bias=1.0), r=["lg"], w=["lg"])
                        S.dve(lambda e: e.tensor_scalar(out=lg[:], in0=lg[:], scalar1=-1.0, scalar2=None, op0=ALU.mult),
                              r=["lg"], w=["lg"])
                        S.dve(lambda e: e.tensor_copy(out=lgq[0:64, :], in_=lg[0:64, 0:4]), r=["lg"], w=["lgq"])
                        S.dve(lambda e: e.tensor_copy(out=lgq[64:128, :], in_=lg[64:128, 4:8]), r=["lg"], w=["lgq"])
                        S.act(lambda e: e.activation(out=decs[:], in_=lgq[:], func=AF.Exp, scale=128.0),
                              r=["lgq"], w=["decs"])
                        S.dve(lambda e: e.tensor_tensor(out=deck[:], in0=lg[:], in1=poskt[:], op=ALU.mult),
                              r=["lg", "poskt"], w=["deck"])
                        S.act(lambda e: e.activation(out=deck[:], in_=deck[:], func=AF.Exp), r=["deck"], w=["deck"])
                        for h in range(4):
                            S.act(lambda e, h=h: e.activation(out=decq[:, h, :], in_=posqt[:], func=AF.Exp,
                                                              scale=lgq[:, h:h + 1]), r=["posqt", "lgq"], w=["decq"])
                            S.act(lambda e, h=h: e.activation(out=dtmp[:, 0, :], in_=ctt[:, 0, :], func=AF.Exp,
                                                              scale=lg[:, h:h + 1]), r=["ctt", "lg"], w=["dtmp0"])
                            S.act(lambda e, h=h: e.activation(out=dtmp[:, 1, :], in_=ctt[:, 2, :], func=AF.Exp,
                                                              scale=lg[:, 4 + h:5 + h]), r=["ctt", "lg"], w=["dtmp1"])
                            S.dve(lambda e: e.tensor_tensor(out=dtmp[:, 0, :], in0=dtmp[:, 0, :], in1=ctt[:, 1, :],
                                                            op=ALU.mult), r=["dtmp0", "ctt"], w=["dtmp0"])
                            S.dve(lambda e: e.tensor_tensor(out=dtmp[:, 1, :], in0=dtmp[:, 1, :], in1=ctt[:, 3, :],
                                                            op=ALU.mult), r=["dtmp1", "ctt"], w=["dtmp1"])
                            S.dve(lambda e, h=h: e.tensor_tensor(out=DT[:, h, :], in0=dtmp[:, 0, :], in1=dtmp[:, 1, :],
                                                                 op=ALU.add), r=["dtmp0", "dtmp1"], w=["DT"])
                        S.dve(lambda e: e.tensor_scalar(out=DT[:], in0=DT[:], scalar1=0.125, scalar2=None, op0=ALU.mult),
                              r=["DT"], w=["DT"])
                        S.dve(lambda e: e.tensor_scalar(out=decq[:], in0=decq[:], scalar1=0.125, scalar2=None,
                                                        op0=ALU.mult), r=["decq"], w=["decq"])
                        S.dve(lambda e: e.memset(deckf[:], 1.0), w=["deckf"])
                        S.dve(lambda e: e.memset(decsf[:], 1.0), w=["decsf"])
                        for h in range(4):
                            for d in range(2):
                                S.dve(lambda e, h=h, d=d: e.tensor_scalar(
                                    out=deckf[:, h, d, :], in0=deckf[:, h, d, :], scalar1=deck[:, d * 4 + h:d * 4 + h + 1],
                                    scalar2=None, op0=ALU.mult), r=["deckf", "deck"], w=["deckf"])
                            S.dve(lambda e, h=h: e.tensor_scalar(
                                out=decsf[:, h * 128:(h + 1) * 128], in0=decsf[:, h * 128:(h + 1) * 128],
                                scalar1=decs[:, h:h + 1], scalar2=None, op0=ALU.mult), r=["decsf", "decs"], w=["decsf"])
                        cnt[0] = len(rjobs)
                        for t in range(TT):
                            b = cnt[0] % 3
                            cnt[0] += 1
                            for k in range(KC):
                                S.pe(lambda e, b=b, k=k, t=t: e.matmul(
                                    pq[b][:, :], lhsT=hT[:, k, t * 128:(t + 1) * 128], rhs=ws[2][:, k, :],
                                    start=(k == 0), stop=(k == KC - 1)), r=[("ws", 2), ("hT", t)], w=[("pq", b)])
                            S.act(lambda e, b=b, t=t: e.activation(out=rv[:, t, :], in_=pq[b][:, :], func=AF.Copy),
                                  r=[("pq", b)], w=[("rv", t)])
                        S.emit()
                    if stop_after == "retproj":
                        return nc
                    Sall = alloc(ret, "Sall", [128, TT, 512], BF16)
                    wg = alloc(ret, "wg", [128, KC, 512], BF16)
                    with contextlib.ExitStack() as ph:
                        S = Sched(nc)
                        Srun = alloc(ph, "Srun", [128, 512], F32)
                        S.dma("pool", lambda e: e.dma_start(out=wg[:], in_=w1v(l, RG, 512)), w=["wg"])
                        pkt = [palloc(ph, f"pkt{i}", [128, 1024], BF16) for i in range(2)]
                        pds = [palloc(ph, f"pds{i}", [128, 512], F32) for i in range(2)]
                        ktall = alloc(ph, "ktall", [128, TT, 4, 2, 64], BF16)
                        for ch in range(TT):
                            pb = ch % 2
                            for j in range(2):
                                S.pe(lambda e, j=j, ch=ch, pb=pb: e.transpose(
                                    out=pkt[pb][:, j * 128:(j + 1) * 128], in_=rkT[:, j, ch * 128:(ch + 1) * 128],
                                    identity=identb[:]), r=["identb"], w=[("pkt", pb)])
                            for d in range(2):
                                S.dve(lambda e, d=d, ch=ch, pb=pb: e.tensor_tensor(
                                    out=ktall[:, ch, :, d, :], in0=pkt[pb][:, 0:256].rearrange("p (h x) -> p h x", h=4),
                                    in1=deckf[:, :, d, :], op=ALU.mult), r=[("pkt", pb), "deckf"], w=[("kt", ch)])
                        S.dve(lambda e: e.memset(Srun[:], 0.0), w=["Srun0", "Srun1"])
                        fwd_order = list(range(TT))
                        bwd_order = list(range(CT - 1, -1, -1)) + list(range(TT - 1, CT - 1, -1))
                        orders = (fwd_order, bwd_order)
                        for ci in range(TT):
                            for half in range(2):
                                ch = orders[half][ci]
                                p0, p1 = half * 64, half * 64 + 64
                                sk = f"Srun{half}"
                                S.act(lambda e, ch=ch, p0=p0, p1=p1: e.activation(out=Sall[p0:p1, ch, :], in_=Srun[p0:p1, :],
                                                                                  func=AF.Copy),
                                      r=[sk], w=[("Sall", ch, half)])
                                if ci == TT - 1:
                                    continue
                                for h in range(4):
                                    S.pe(lambda e, h=h, ch=ch, half=half: e.matmul(
                                        pds[half][:, h * 128:(h + 1) * 128], lhsT=ktall[:, ch, h, :, :],
                                        rhs=rv[:, ch, h * 128:(h + 1) * 128], start=True, stop=True),
                                        r=[("kt", ch)], w=[("pds", half)])
                                S.dve(lambda e, p0=p0, p1=p1: e.tensor_tensor(
                                    out=Srun[p0:p1, :], in0=Srun[p0:p1, :], in1=decsf[p0:p1, :], op=ALU.mult),
                                    r=[sk, "decsf"], w=[sk])
                                S.dve(lambda e, p0=p0, p1=p1, half=half: e.tensor_tensor(
                                    out=Srun[p0:p1, :], in0=Srun[p0:p1, :], in1=pds[half][p0:p1, :], op=ALU.add),
                                    r=[sk, ("pds", half)], w=[sk])
                        S.emit()
                    ssall = alloc(ret, "ssall", [128, TT * 4], F32)
                    uall = alloc(ret, "uall", [128, TT, 512], BF16)
                    with contextlib.ExitStack() as ph:
                        S = Sched(nc)
                        pa = [[palloc(ph, f"pa{i}{j}", [128, 512], F32) for j in range(2)] for i in range(2)]
                        po = [palloc(ph, f"po{i}", [128, 512], F32) for i in range(2)]
                        pg = [palloc(ph, f"pg{i}", [128, 512], F32) for i in range(2)]
                        osb = [alloc(ph, f"osb{i}", [128, 512], BF16) for i in range(2)]
                        at = [Pw[:, i, :].rearrange("p (h n) -> p h n", h=4) for i in range(2)]
                        qt_ = [Pw[:, 2 + i, :].rearrange("p (h n) -> p h n", h=4) for i in range(2)]
                        sg = [alloc(ph, f"sg{i}", [128, 512], F32) for i in range(2)]
                        junk2 = alloc(ph, "junk2", [128, 128], F32)

                        def stA(ch, b):
                            tk = slice(ch * 128, (ch + 1) * 128)
                            S.dve(lambda e: e.tensor_tensor(out=qt_[b][:], in0=rqT[:, :, tk], in1=decq[:], op=ALU.mult),
                                  w=[("qt_", b)])
                            for h in range(4):
                                hc, hh = h // 2, (h % 2) * 64
                                S.pe(lambda e, h=h, hc=hc, hh=hh: e.matmul(
                                    pa[b][h % 2][:, hc * 128:(hc + 1) * 128], lhsT=rkT[hh:hh + 64, hc, tk],
                                    rhs=rqT[hh:hh + 64, h, tk], start=True, stop=True), w=[("pa", b, h % 2)])
                            for k in range(KC):
                                S.pe(lambda e, k=k: e.matmul(pg[b][:], lhsT=hT[:, k, tk], rhs=wg[:, k, :],
                                                             start=(k == 0), stop=(k == KC - 1)), r=["wg"], w=[("pg", b)])

                        def stB(ch, b, ci):
                            tk = slice(ch * 128, (ch + 1) * 128)
                            for par in range(2):
                                S.dve(lambda e, par=par: e.tensor_tensor(
                                    out=at[b][:, par::2, :], in0=pa[b][par][:, 0:256].rearrange("p (h n) -> p h n", h=2),
                                    in1=DT[:, par::2, :], op=ALU.mult), r=[("pa", b, par)], w=[("at", b)])
                            for h in range(4):
                                S.pe(lambda e, h=h: e.matmul(
                                    po[b][:, h * 128:(h + 1) * 128], lhsT=at[b][:, h, :], rhs=rv[:, ch, h * 128:(h + 1) * 128],
                                    start=True, stop=False), r=[("at", b)], w=[("po", b)])
                                S.pe(lambda e, h=h: e.matmul(
                                    po[b][:, h * 128:(h + 1) * 128], lhsT=qt_[b][:, h, :], rhs=Sall[:, ch, h * 128:(h + 1) * 128],
                                    start=False, stop=True), r=[("qt_", b)], w=[("po", b)])
                            S.act(lambda e: e.activation(out=osb[b][:], in_=po[b][:], func=AF.Copy), r=[("po", b)], w=[("osb", b)])
                            S.act(lambda e: e.activation(out=sg[b][:], in_=pg[b][:], func=AF.Silu), r=[("pg", b)], w=[("sg", b)])
                            for h in range(4):
                                S.dve(lambda e, h=h: e.scalar_tensor_tensor(
                                    out=junk2[:], in0=osb[b][:, h * 128:(h + 1) * 128], scalar=1.0,
                                    in1=osb[b][:, h * 128:(h + 1) * 128], op0=ALU.mult, op1=ALU.mult,
                                    accum_out=ssall[:, ci * 4 + h:ci * 4 + h + 1]),
                                    r=[("osb", b)], w=["junk2", ("ssall", ci)])
                            S.dve(lambda e: e.tensor_tensor(out=uall[:, ci, :], in0=osb[b][:], in1=sg[b][:], op=ALU.mult),
                                  r=[("osb", b), ("sg", b)], w=[("uall", ci)])

                        def stC(ch, b, ci):
                            tk = slice(ch * 128, (ch + 1) * 128)
                            for h in range(4):
                                S.dve(lambda e, h=h: e.tensor_scalar(
                                    out=ytm[b][:, h * 128:(h + 1) * 128], in0=uall[:, ci, h * 128:(h + 1) * 128],
                                    scalar1=ssall[:, ci * 4 + h:ci * 4 + h + 1], scalar2=None, op0=ALU.mult),
                                    r=[("uall", ci), "rstd"], w=[("ytm", b, h)])
                            for j in range(4):
                                S.pe(lambda e, j=j: e.transpose(out=pyt[b][:, j * 128:(j + 1) * 128],
                                                                in_=ytm[b][:, j * 128:(j + 1) * 128], identity=identb[:]),
                                     r=[("ytm", b, j)], w=[("pyt", b)])
                            S.act(lambda e: e.activation(out=yT[:, 0, :, tk],
                                                         in_=pyt[b][:, 0:512].rearrange("p (j n) -> p j n", j=4),
                                                         func=AF.Copy), r=[("pyt", b)], w=[("yT0", ci)])

                        nq = len(qtiles)
                        for step in range(nq + 1):
                            if step < nq:
                                stA(qtiles[step], step % 2)
                            if 0 <= step - 1 < nq:
                                stB(qtiles[step - 1], (step - 1) % 2, step - 1)
                        S.emit()
                    with contextlib.ExitStack() as ph:
                        S = Sched(nc)
                        pyt = [palloc(ph, f"pyt{i}", [128, 1024], BF16) for i in range(2)]
                        ytm = [alloc(ph, f"ytmb{i}", [128, 512], BF16) for i in range(2)]
                        S.dma("pool", lambda e: e.dma_start(out=Pw[:], in_=w1v(l, NQ, 512)), w=["Pw"])
                        S.act(lambda e: e.activation(out=ssall[:, 0:nq * 4], in_=ssall[:, 0:nq * 4], func=AF.Sqrt,
                                                     scale=1.0 / 128, bias=EPS),
                              r=[("ssall", ci) for ci in range(nq)], w=["rstd"])
                        S.dve(lambda e: e.reciprocal(out=ssall[:, 0:nq * 4], in_=ssall[:, 0:nq * 4]), r=["rstd"], w=["rstd"])
                        for ci in range(nq):
                            stC(qtiles[ci], ci % 2, ci)
                        S.emit()
                if stop_after == "ret":
                    return nc, yT
                with contextlib.ExitStack() as nast:
                  nqT = alloc(nast, "nqT", [128, 4, NT], BF16)
                  nkT = alloc(nast, "nkT", [128, 8, NT], BF16)
                  nv = alloc(nast, "nv", [128, TT, 8, 65], BF16)
                  with contextlib.ExitStack() as ph:
                    S = Sched(nc)
                    ws = [Pw] + [alloc(ph, f"nws{i}", [128, KC, 512], BF16) for i in range(2)]
                    pq = [palloc(ph, f"npq{i}", [128, 512], F32) for i in range(3)]
                    S.dma("pool", lambda e: e.dma_start(out=ws[1][:], in_=w1v(l, NK, 512)), w=[("ws", 1)])
                    S.dma("pool", lambda e: e.dma_start(out=ws[2][:], in_=w1v(l, NV, 512)), w=[("ws", 2)])
                    S.dve(lambda e: e.memset(nv[:, :, :, 64:65], 1.0), w=["nv1"])
                    S.dve(lambda e: e.memset(nkT[64:128, 0::2, :], 0.0), w=["nkz"])
                    S.dve(lambda e: e.memset(nkT[0:64, 1::2, :], 0.0), w=["nkz"])
                    pcnt = 0
                    for wi in (0, 1):
                        for j in range(4):
                            for (s0, n, isc) in tok_blocks(c):
                                b = pcnt % 3
                                pcnt += 1
                                for k in range(KC):
                                    S.pe(lambda e, b=b, k=k, s0=s0, n=n, wi=wi, j=j: e.matmul(
                                        pq[b][:, 0:n], lhsT=ws[wi][:, k, j * 128:(j + 1) * 128], rhs=hT[:, k, s0:s0 + n],
                                        start=(k == 0), stop=(k == KC - 1)), r=[("ws", wi)], w=[("pq", b)])
                                if wi == 0:
                                    S.act(lambda e, b=b, s0=s0, n=n, j=j: e.activation(
                                        out=nqT[:, j, s0:s0 + n], in_=pq[b][:, 0:n], func=AF.Copy),
                                        r=[("pq", b)], w=[("nqk", 0, j, s0)])
                                else:
                                    S.act(lambda e, b=b, s0=s0, n=n, j=j: e.activation(
                                        out=nkT[0:64, 2 * j, s0:s0 + n], in_=pq[b][0:64, 0:n], func=AF.Copy),
                                        r=[("pq", b), "nkz"], w=[("nqk", 1, j, s0)])
                                    S.dve(lambda e, b=b, s0=s0, n=n, j=j: e.tensor_copy(
                                        out=nkT[64:128, 2 * j + 1, s0:s0 + n], in_=pq[b][64:128, 0:n]),
                                        r=[("pq", b), "nkz"], w=[("nqk", 2, j, s0)])
                    for t in range(TT):
                        b = pcnt % 3
                        pcnt += 1
                        for k in range(KC):
                            S.pe(lambda e, b=b, k=k, t=t: e.matmul(
                                pq[b][:, :], lhsT=hT[:, k, t * 128:(t + 1) * 128], rhs=ws[2][:, k, :],
                                start=(k == 0), stop=(k == KC - 1)), r=[("ws", 2)], w=[("pq", b)])
                        S.act(lambda e, b=b, t=t: e.activation(out=nv[:, t, :, 0:64],
                                                               in_=pq[b][:, :].rearrange("p (h d) -> p h d", h=8),
                                                               func=AF.Copy), r=[("pq", b)], w=[("nv", t)])
                    S.emit()
                  with contextlib.ExitStack() as ph:
                    S = Sched(nc)
                    Et = alloc(ph, "Et", [128, 8, 17, 64], BF16)
                    Ef = [alloc(ph, f"Ef{i}", [128, 17 * 64], F32) for i in range(2)]
                    pq = [palloc(ph, f"npo{i}", [128, 512], F32) for i in range(2)]
                    pw_chunks = list(range(KC))
                    for h in range(8):
                        eb = h % 2
                        for u in range(2):
                            S.dma("sp", lambda e, h=h, eb=eb, u=u: e.dma_start(
                                out=Ef[eb][u * 64:(u + 1) * 64, :], in_=nab[l, :, h * 1088:(h + 1) * 1088]),
                                w=[("Ef", eb, u)])
                        S.act(lambda e, h=h, eb=eb: e.activation(out=Et[:, h, :, :].rearrange("p b c -> p (b c)"),
                                                                 in_=Ef[eb][:], func=AF.Exp),
                              r=[("Ef", eb, 0), ("Ef", eb, 1)], w=[("Et", h)])
                    NB, LA = 4, 3
                    pst = [palloc(ph, f"pst{i}", [128, 512], F32) for i in range(NB)]
                    ppo = pq
                    ppt = palloc(ph, "ppt", [128, 1024], BF16)
                    ptb = [alloc(ph, f"ptb{i}", [128, 4, 128], BF16) for i in range(NB)]
                    ytm = alloc(ph, "nytm", [128, 512], BF16)
                    rc = alloc(ph, "nrc", [128, 8, 1], F32)
                    its = []
                    for qt in qtiles:
                        isq_ctx = qt < CT
                        if isq_ctx:
                            ntl, lt0, R0, r, nrows = 0, 0, 0, 0, 0
                        else:
                            r = (qt - CT) * 2
                            r0a = min(max(r - 4, 0), c.R - 8)
                            r0b = min(max(r + 1 - 4, 0), c.R - 8)
                            R0 = r0a
                            nrows = r0b + 8 - r0a
                            ntl = (nrows + 1) // 2
                            lt0 = R0 // 2
                        blocks = [(t, CT + lt0 + t, True) for t in range(ntl)] + [(None, kt_, False) for kt_ in range(CT)]
                        parts = [blocks[i_:i_ + 4] for i_ in range(0, len(blocks), 4)]
                        for h in range(8):
                            for pi, part in enumerate(parts):
                                its.append(dict(qt=qt, h=h, isq_ctx=isq_ctx, ntl=ntl, R0=R0, r=r, nrows=nrows, part=part,
                                                first=(pi == 0), last=(pi == len(parts) - 1)))

                    def emit_S(i):
                        it = its[i]
                        h = it["h"]
                        hc = h // 2
                        b = i % NB
                        tq = slice(it["qt"] * 128, (it["qt"] + 1) * 128)
                        for j, (t_, ktile, isloc) in enumerate(it["part"]):
                            S.pe(lambda e, j=j, ktile=ktile: e.matmul(
                                pst[b][:, j * 128:(j + 1) * 128],
                                lhsT=nkT[:, h, ktile * 128:(ktile + 1) * 128],
                                rhs=nqT[:, hc, tq], start=True, stop=True),
                                w=[("pst", b)])

                    for i0 in range(min(LA, len(its))):
                        emit_S(i0)
                    for i in range(len(its)):
                        it = its[i]
                        qt, h, isq_ctx, ntl, R0, r, nrows, part = (it[k] for k in
                                                                   ("qt", "h", "isq_ctx", "ntl", "R0", "r", "nrows", "part"))
                        nb = len(part)
                        b = i % NB
                        tq = slice(qt * 128, (qt + 1) * 128)
                        if i + LA < len(its):
                            emit_S(i + LA)
                        if pw_chunks and i >= 6 and (i % 6 == 0 or i == len(its) - 1):
                            for kq in ([pw_chunks.pop(0)] if i < len(its) - 1 else list(pw_chunks)):
                                S.dma("pool", lambda e, kq=kq: e.dma_start(out=Pw[:, kq, :], in_=w1v(l, GQ, 512)[:, kq, :]),
                                      w=[("Pw", kq)])
                            if i == len(its) - 1:
                                pw_chunks = []
                        pkeys = [("ptb", b, u, qo) for u in range(2) for qo in range(2)] + [("ptbz", b, z) for z in range(4)]
                        S.act(lambda e, b=b, nb=nb: e.activation(
                            out=ptb[b][:, 0:nb, :], in_=pst[b][:, 0:nb * 128].rearrange("p (a n) -> p a n", a=nb),
                            func=AF.Exp, scale=0.125), r=[("pst", b)], w=pkeys)
                        loc = [(j, t_) for j, (t_, ktile, isloc) in enumerate(part) if isloc]
                        if loc:
                            zc = 0
                            jl0 = loc[0][0]
                            tl = [t_ for (_, t_) in loc]
                            for u in range(2):
                                for qo in range(2):
                                    qr = r + qo
                                    r0q = min(max(qr - 4, 0), c.R - 8)
                                    valid = [t for t in tl if r0q <= R0 + 2 * t + u <= r0q + 7]
                                    inval = [t for t in tl if t not in valid
                                             and not (u == 1 and t == ntl - 1 and nrows % 2 == 1)]
                                    if valid:
                                        ta, tb_ = valid[0], valid[-1] + 1
                                        assert valid == list(range(ta, tb_))
                                        j0 = R0 + 2 * ta + u - qr + 8
                                        ja = jl0 + (ta - tl[0])
                                        jb = ja + (tb_ - ta)
                                        S.add("pool" if u == 1 else "dve",
                                              lambda e, b=b, u=u, qo=qo, ja=ja, jb=jb, j0=j0, h=h, ta=ta, tb_=tb_: e.tensor_tensor(
                                            out=ptb[b][u * 64:(u + 1) * 64, ja:jb, qo * 64:(qo + 1) * 64],
                                            in0=ptb[b][u * 64:(u + 1) * 64, ja:jb, qo * 64:(qo + 1) * 64],
                                            in1=Et[u * 64:(u + 1) * 64, h, j0:j0 + 2 * (tb_ - ta):2, :], op=ALU.mult),
                                            [("ptb", b, u, qo), ("Et", h)], [("ptb", b, u, qo)])
                                    for t in inval:
                                        jz = jl0 + (t - tl[0])
                                        S.add("pool", lambda e, b=b, u=u, qo=qo, jz=jz: e.memset(
                                            ptb[b][u * 64:(u + 1) * 64, jz, qo * 64:(qo + 1) * 64], 0.0),
                                            (), [("ptbz", b, zc)])
                                        zc += 1
                            assert zc <= 4
                        pb = h // 4
                        for j, (t_, ktile, isloc) in enumerate(part):
                            K = 128
                            if isloc and t_ == ntl - 1 and (nrows % 2 == 1):
                                K = 64
                            S.pe(lambda e, b=b, j=j, ktile=ktile, h=h, pb=pb, K=K, nb=nb, it=it: e.matmul(
                                ppo[pb][:, (h % 4) * 65:(h % 4) * 65 + 65], lhsT=ptb[b][0:K, j, :],
                                rhs=nv[0:K, ktile, h, :], start=(it["first"] and j == 0), stop=(it["last"] and j == nb - 1)),
                                r=pkeys + ["nv", "nv1"], w=[("pq", pb)])
                        if h == 7 and it["last"]:
                            for pb in range(2):
                                S.dve(lambda e, pb=pb: e.reciprocal(
                                    out=rc[:, pb * 4:(pb + 1) * 4, :],
                                    in_=ppo[pb][:, 0:260].rearrange("p (h d) -> p h d", h=4)[:, :, 64:65]),
                                    r=[("pq", pb)], w=[("rc", pb)])
                                for hq in range(4):
                                    h2 = pb * 4 + hq
                                    S.dve(lambda e, pb=pb, hq=hq, h2=h2: e.tensor_scalar(
                                        out=ytm[:, h2 * 64:(h2 + 1) * 64], in0=ppo[pb][:, hq * 65:hq * 65 + 64],
                                        scalar1=rc[:, h2, :], scalar2=None, op0=ALU.mult),
                                        r=[("pq", pb), ("rc", pb)], w=[("ytm", h2)])
                            for j in range(4):
                                S.pe(lambda e, j=j: e.transpose(out=ppt[:, j * 128:(j + 1) * 128],
                                                                in_=ytm[:, j * 128:(j + 1) * 128], identity=identb[:]),
                                     r=[("ytm", 2 * j), ("ytm", 2 * j + 1)], w=["ppt"])
                            S.act(lambda e, tq=tq: e.activation(out=yT[:, 1, :, tq],
                                                                in_=ppt[:, 0:512].rearrange("p (j n) -> p j n", j=4),
                                                                func=AF.Copy), r=["ppt"], w=[("yT1", qt)])
                    S.emit()
                if stop_after == "na":
                    return nc, yT
                with contextlib.ExitStack() as ph:
                    S = Sched(nc)
                    gqT = alloc(ph, "gqT", [128, 4, NT], BF16)
                    gkT = alloc(ph, "gkT", [128, 2, NT], BF16)
                    gvA = alloc(ph, "gvA", [128, TT, 2, 128], BF16)
                    gvB = alloc(ph, "gvB", [128, TT, 2, 128], BF16)
                    cosT = alloc(ph, "gcosT", [128, NL], BF16)
                    sinT = alloc(ph, "gsinT", [128, NL], BF16)
                    gain = alloc(ph, "gain", [128, 2], F32)
                    S.dma("pool", lambda e: e.dma_start(out=cosT[:], in_=cosd), w=["cosT"])
                    S.dma("pool", lambda e: e.dma_start(out=sinT[:], in_=sind), w=["sinT"])
                    S.dma("sp", lambda e: e.dma_start(out=gain[:], in_=gqn[l].rearrange("a p -> p a"),
                                                      allow_slow_non_contiguous=True), w=["gain"])
                    ws = [Pw, alloc(ph, "gws1", [128, KC, 512], BF16)]
                    S.dma("pool", lambda e: e.dma_start(out=ws[1][:, :, 0:256], in_=w1v(l, GK, 256)), w=[("ws", 1)])
                    S.dve(lambda e: e.memset(gvA[:, :, :, 64:128], 0.0), w=["gv1"])
                    S.dve(lambda e: e.memset(gvA[:, :, :, 64:65], 1.0), w=["gv1"])
                    S.dve(lambda e: e.memset(gkT[64:128, 0, :], 0.0), w=["gkz"])
                    S.dve(lambda e: e.memset(gkT[0:64, 1, :], 0.0), w=["gkz"])
                    S.dve(lambda e: e.memset(gvB[:, :, :, 0:64], 0.0), w=["gv1"])
                    S.dve(lambda e: e.memset(gvB[:, :, :, 0:1], 1.0), w=["gv1"])
                    pq = [palloc(ph, f"gpq{i}", [128, 512], F32) for i in range(2)]
                    pr = [palloc(ph, f"gpr{i}", [128, 512], F32) for i in range(2)]
                    sq = [alloc(ph, f"gsq{i}", [128, 512], BF16) for i in range(2)]
                    rs = [alloc(ph, f"grs{i}", [128, 512], F32) for i in range(2)]
                    qn = [alloc(ph, f"gqn{i}", [128, 512], BF16) for i in range(2)]
                    t1 = [alloc(ph, f"gt1{i}", [128, 512], BF16) for i in range(2)]
                    t2 = [alloc(ph, f"gt2{i}", [128, 512], BF16) for i in range(2)]
                    gc = 0
                    jobs = [(0, j * 128, (lambda s0, n, j=j: gqT[:, j, s0:s0 + n]), 0, not last) for j in range(4)]
                    jobs.append((1, 0, None, 1, True))
                    gjobs = []
                    for (wi, col, dest_fn, gi, wctx) in jobs:
                        for (s0, n, isc) in tok_blocks(c, with_ctx=wctx):
                            gjobs.append((wi, col, dest_fn, gi, s0, n, isc))

                    def g_p1(i):
                        wi, col, dest_fn, gi, s0, n, isc = gjobs[i]
                        b = i % 2
                        for k in range(KC):
                            S.pe(lambda e, k=k: e.matmul(
                                pq[b][:, 0:n], lhsT=ws[wi][:, k, col:col + 128], rhs=hT[:, k, s0:s0 + n],
                                start=(k == 0), stop=(k == KC - 1)), r=[("ws", wi)], w=[("pq", b)])
                        S.act(lambda e: e.activation(out=sq[b][:, 0:n], in_=pq[b][:, 0:n], func=AF.Square),
                              r=[("pq", b)], w=[("sq", b)])

                    def g_p2(i):
                        wi, col, dest_fn, gi, s0, n, isc = gjobs[i]
                        b = i % 2
                        S.pe(lambda e: e.matmul(pr[b][:, 0:n], lhsT=bonesb[:], rhs=sq[b][:, 0:n], start=True, stop=True),
                             r=[("sq", b)], w=[("pr", b)])
                        S.act(lambda e: e.activation(out=rs[b][:, 0:n], in_=pr[b][:, 0:n], func=AF.Ln, scale=1.0 / 64, bias=EPS),
                              r=[("pr", b)], w=[("rs", b)])
                        S.act(lambda e: e.activation(out=rs[b][:, 0:n], in_=rs[b][:, 0:n], func=AF.Exp, scale=-0.5),
                              r=[("rs", b)], w=[("rs", b)])
                        if isc and dest_fn is None:
                            for hf in range(2):
                                S.dve(lambda e, hf=hf: e.scalar_tensor_tensor(
                                    out=gkT[hf * 64:(hf + 1) * 64, hf, s0:s0 + n], in0=pq[b][hf * 64:(hf + 1) * 64, 0:n],
                                    scalar=gain[hf * 64:(hf + 1) * 64, gi:gi + 1], in1=rs[b][hf * 64:(hf + 1) * 64, 0:n],
                                    op0=ALU.mult, op1=ALU.mult), r=[("pq", b), ("rs", b), "gain", "gkz"], w=[("gqk", i)])
                        elif isc:
                            S.dve(lambda e: e.scalar_tensor_tensor(
                                out=dest_fn(s0, n), in0=pq[b][:, 0:n], scalar=gain[:, gi:gi + 1], in1=rs[b][:, 0:n],
                                op0=ALU.mult, op1=ALU.mult), r=[("pq", b), ("rs", b), "gain"], w=[("gqk", i)])
                        else:
                            S.dve(lambda e: e.scalar_tensor_tensor(
                                out=qn[b][:, 0:n], in0=pq[b][:, 0:n], scalar=gain[:, gi:gi + 1], in1=rs[b][:, 0:n],
                                op0=ALU.mult, op1=ALU.mult), r=[("pq", b), ("rs", b), "gain"], w=[("qn", b)])

                    def g_p3(i):
                        wi, col, dest_fn, gi, s0, n, isc = gjobs[i]
                        b = i % 2
                        if isc:
                            return
                        l0 = s0 - CTX
                        S.pe(lambda e: e.matmul(pr[b][:, 0:n], lhsT=permb[:], rhs=qn[b][:, 0:n], start=True, stop=True),
                             r=[("qn", b)], w=[("pr", b)])
                        S.dve(lambda e: e.tensor_tensor(out=t1[b][:, 0:n], in0=qn[b][:, 0:n], in1=cosT[:, l0:l0 + n],
                                                        op=ALU.mult), r=[("qn", b), "cosT"], w=[("t1", b)])
                        S.dve(lambda e: e.tensor_tensor(out=t2[b][:, 0:n], in0=pr[b][:, 0:n], in1=sinT[:, l0:l0 + n],
                                                        op=ALU.mult), r=[("pr", b), "sinT"], w=[("t2", b)])
                        if dest_fn is None:
                            for hf in range(2):
                                S.dve(lambda e, hf=hf: e.tensor_tensor(
                                    out=gkT[hf * 64:(hf + 1) * 64, hf, s0:s0 + n],
                                    in0=t1[b][hf * 64:(hf + 1) * 64, 0:n], in1=t2[b][hf * 64:(hf + 1) * 64, 0:n],
                                    op=ALU.add), r=[("t1", b), ("t2", b), "gkz"], w=[("gqk", i)])
                        else:
                            S.dve(lambda e: e.tensor_tensor(out=dest_fn(s0, n), in0=t1[b][:, 0:n], in1=t2[b][:, 0:n],
                                                            op=ALU.add), r=[("t1", b), ("t2", b)], w=[("gqk", i)])

                    ng = len(gjobs)
                    for step in range(ng + 2):
                        if 0 <= step - 2 < ng:
                            g_p3(step - 2)
                        if 0 <= step - 1 < ng:
                            g_p2(step - 1)
                        if step < ng:
                            g_p1(step)
                    gc = ng
                    for t in range(TT):
                        b = gc % 2
                        gc += 1
                        for k in range(KC):
                            S.pe(lambda e, b=b, k=k, t=t: e.matmul(
                                pq[b][:, 0:128], lhsT=hT[:, k, t * 128:(t + 1) * 128], rhs=ws[1][:, k, 128:256],
                                start=(k == 0), stop=(k == KC - 1)), r=[("ws", 1)], w=[("pq", b)])
                        S.act(lambda e, b=b, t=t: e.activation(out=gvA[:, t, :, 0:64],
                                                               in_=pq[b][:, 0:128].rearrange("p (g d) -> p g d", g=2),
                                                               func=AF.Copy), r=[("pq", b)], w=[("gvA", t)])
                        S.act(lambda e, b=b, t=t: e.activation(out=gvB[:, t, :, 64:128],
                                                               in_=pq[b][:, 0:128].rearrange("p (g d) -> p g d", g=2),
                                                               func=AF.Copy), r=[("pq", b)], w=[("gvB", t)])
                    for a_, cbase in enumerate((GA, GB, GC)):
                        S.dma("pool", lambda e, a_=a_, cbase=cbase: e.dma_start(
                            out=Pw[:, :, a_ * 128:(a_ + 1) * 128], in_=w1v(l, cbase, 128)), w=[("ws", 0)])
                    pst = [palloc(ph, f"gpst{i}", [128, 512], F32) for i in range(3)]
                    pbc = palloc(ph, "gpbc", [128, 512], F32)
                    ppo = [pq[0], pq[1], pr[0], pr[1]]
                    pok = [("pq", 0), ("pq", 1), ("pr", 0), ("pr", 1)]
                    ptb = [alloc(ph, f"gptb{i}", [128, 512], BF16) for i in range(3)]
                    rd = [alloc(ph, f"grd{i}", [128, 256], F32) for i in range(2)]
                    bcs = [alloc(ph, f"gbcs{i}", [128, 256], F32) for i in range(2)]
                    onesf = alloc(ph, "gonesf", [128, 128], F32)
                    S.dve(lambda e: e.memset(onesf[:], 0.0), w=["onesf"])
                    S.dve(lambda e: e.memset(onesf[64:65, 0:64], 1.0), w=["onesf"])
                    S.dve(lambda e: e.memset(onesf[0:1, 64:128], 1.0), w=["onesf"])
                    for i_ in range(2):
                        S.dve(lambda e, i_=i_: e.memset(rd[i_][:], 0.0), w=[("rdA", i_), ("rdB", i_)])
                    its = []
                    grp = 0
                    for qt in qtiles:
                        ktiles = list(range(CT)) if qt < CT else list(range(TT))
                        for g in range(2):
                            for ki, ktile in enumerate(ktiles):
                                its.append((qt, g, ki, len(ktiles), ktile, grp))
                            grp += 1

                    def emit_S(i):
                        qt, g, ki, nk, ktile, gn = its[i]
                        b = i % 3
                        g0 = g * 64
                        tq = slice(qt * 128, (qt + 1) * 128)
                        S.pe(lambda e: e.matmul(
                            pst[b][:, :], lhsT=gkT[:, g, ktile * 128:(ktile + 1) * 128],
                            rhs=gqT[:, :, tq], start=True, stop=True), r=[("gqk", i_) for i_ in range(ng)],
                            w=[("pst", b)])

                    emit_S(0)
                    if len(its) > 1:
                        emit_S(1)
                    deferred = []
                    for i in range(len(its)):
                        qt, g, ki, nk, ktile, gn = its[i]
                        b = i % 3
                        if i + 2 < len(its):
                            emit_S(i + 2)
                        S.act(lambda e, b=b: e.activation(out=ptb[b][:], in_=pst[b][:], func=AF.Exp, scale=0.125),
                              r=[("pst", b)], w=[("ptb", b)])
                        pa_, pb_ = 2 * (gn % 2), 2 * (gn % 2) + 1
                        if ki == 0:
                            for d in [d for d in deferred if d[2] == gn % 2]:
                                d[1]()
                            deferred[:] = [d for d in deferred if d[2] != gn % 2]
                        pv = ptb[b][:].rearrange("p (j q) -> p j q", j=4)
                        S.pe(lambda e, pa_=pa_, pv=pv, ktile=ktile, g=g, ki=ki, nk=nk: e.matmul(
                            ppo[pa_][:, 0:256], lhsT=gvA[:, ktile, g, :], rhs=pv[:, 0::2, :],
                            start=(ki == 0), stop=(ki == nk - 1)), r=[("ptb", b), ("gvA", ktile), "gv1"], w=[pok[pa_]])
                        S.pe(lambda e, pb_=pb_, pv=pv, ktile=ktile, g=g, ki=ki, nk=nk: e.matmul(
                            ppo[pb_][:, 0:256], lhsT=gvB[:, ktile, g, :], rhs=pv[:, 1::2, :],
                            start=(ki == 0), stop=(ki == nk - 1)), r=[("ptb", b), ("gvB", ktile), "gv1"], w=[pok[pb_]])
                        for d in [d for d in deferred if d[0] <= i]:
                            d[1]()
                        deferred[:] = [d for d in deferred if d[0] > i]
                        if ki == nk - 1:
                            rb = gn % 2
                            tq = slice(qt * 128, (qt + 1) * 128)

                            def st1(rb=rb, pa_=pa_, pb_=pb_):
                                S.dve(lambda e: e.reciprocal(out=rd[rb][64:65, :], in_=ppo[pa_][64:65, 0:256]),
                                      r=[pok[pa_]], w=[("rdA", rb)])
                                S.dve(lambda e: e.reciprocal(out=rd[rb][0:1, :], in_=ppo[pb_][0:1, 0:256]),
                                      r=[pok[pb_]], w=[("rdB", rb)])

                            def st2(rb=rb):
                                S.pe(lambda e: e.matmul(pbc[:, 0:256], lhsT=onesf[:], rhs=rd[rb][:],
                                                        start=True, stop=True),
                                     r=[("rdA", rb), ("rdB", rb), "onesf"], w=["pbc"])

                            def st3(rb=rb, pa_=pa_, pb_=pb_, g=g, tq=tq):
                                S.dve(lambda e: e.tensor_copy(out=bcs[rb][:], in_=pbc[:, 0:256]),
                                      r=["pbc"], w=[("bcsA", rb), ("bcsB", rb)])
                                S.dve(lambda e: e.tensor_tensor(
                                    out=yT[0:64, 2, 2 * g:2 * g + 2, tq],
                                    in0=ppo[pa_][0:64, 0:256].rearrange("p (j q) -> p j q", j=2),
                                    in1=bcs[rb][0:64, :].rearrange("p (j q) -> p j q", j=2), op=ALU.mult),
                                    r=[pok[pa_], ("bcsA", rb)], w=[("yT2", 0, g, qt)])
                                S.dve(lambda e: e.tensor_tensor(
                                    out=yT[64:128, 2, 2 * g:2 * g + 2, tq],
                                    in0=ppo[pb_][64:128, 0:256].rearrange("p (j q) -> p j q", j=2),
                                    in1=bcs[rb][64:128, :].rearrange("p (j q) -> p j q", j=2), op=ALU.mult),
                                    r=[pok[pb_], ("bcsB", rb)], w=[("yT2", 1, g, qt)])

                            last_i = len(its) - 1
                            deferred.append((min(i + 1, last_i), st1, rb))
                            deferred.append((min(i + 4, last_i), st2, rb))
                            deferred.append((min(i + 6, last_i), st3, rb))
                    for d in deferred:
                        d[1]()
                    S.emit()
                if stop_after == "gqa":
                    return nc, yT
                with contextlib.ExitStack() as mg:
                    zT = alloc(mg, "zT", [128, KC, NT], BF16)
                    with contextlib.ExitStack() as ph:
                        S = Sched(nc)
                        wga = [alloc(ph, f"wga{i}", [128, KC, 3, 128], BF16) for i in range(2)]
                        wbb = [alloc(ph, f"wbb{i}", [128, 4, 3, 128], BF16) for i in range(2)]
                        pgt = [palloc(ph, f"pgt{i}", [128, 512], F32) for i in range(4)]
                        put = [palloc(ph, f"put{i}", [128, 512], F32) for i in range(4)]
                        sgm = [alloc(ph, f"sgm{i}", [128, 512], F32) for i in range(3)]
                        zac = [alloc(ph, f"zac{i}", [128, 512], F32) for i in range(2)]
                        ztm = [alloc(ph, f"ztm{i}", [128, 512], F32) for i in range(2)]
                        blocks = tok_blocks(c, with_ctx=not last)
                        pc = 0
                        zc = 0
                        Pg = Pw[:, :, 0:384].rearrange("p k (a n) -> p k a n", a=3)
                        for fc in range(KC):
                            wb = fc % 2
                            for a, cbase in enumerate((GA, GB, GC)):
                                if fc > 0:
                                    S.dma("pool", lambda e, wb=wb, a=a, cbase=cbase, fc=fc: e.dma_start(
                                        out=wga[wb][:, :, a, :], in_=w1v(l, cbase + fc * 128, 128)), w=[("wga", wb)])
                                S.dma("pool", lambda e, wb=wb, a=a, fc=fc: e.dma_start(
                                    out=wbb[wb][:, :, a, :],
                                    in_=wbr[l, a].rearrange("(k p) n -> p k n", p=128)[:, :, fc * 128:(fc + 1) * 128]),
                                    w=[("wbb", wb)])
                            if fc == 1:
                                S.dma("pool", lambda e: e.dma_start(
                                    out=Pw[:], in_=wout[l].rearrange("(k p) n -> p k n", p=128)[:, :, 0:512]), w=[("wga", 0)])
                            for (s0, n, isc) in blocks:
                                zb = zc % 2
                                zc += 1
                                for a in range(3):
                                    b = pc % 4
                                    pc += 1
                                    for k in range(KC):
                                        S.pe(lambda e, b=b, k=k, a=a, wb=wb, s0=s0, n=n, fc=fc: e.matmul(
                                            pgt[b][:, 0:n], lhsT=(Pg if fc == 0 else wga[wb])[:, k, a, :], rhs=hT[:, k, s0:s0 + n],
                                            start=(k == 0), stop=(k == KC - 1)), r=[("wga", wb)], w=[("pgt", b)])
                                    for k in range(4):
                                        S.pe(lambda e, b=b, k=k, a=a, wb=wb, s0=s0, n=n: e.matmul(
                                            put[b][:, 0:n], lhsT=wbb[wb][:, k, a, :], rhs=yT[:, a, k, s0:s0 + n],
                                            start=(k == 0), stop=(k == 3)), r=[("wbb", wb)], w=[("put", b)])
                                    S.act(lambda e, b=b, a=a, n=n: e.activation(out=sgm[a][:, 0:n], in_=pgt[b][:, 0:n],
                                                                                func=AF.Sigmoid),
                                          r=[("pgt", b)], w=[("sgm", a)])
                                    if a == 0:
                                        S.dve(lambda e, b=b, n=n, zb=zb: e.tensor_tensor(
                                            out=zac[zb][:, 0:n], in0=sgm[0][:, 0:n], in1=put[b][:, 0:n], op=ALU.mult),
                                            r=[("sgm", 0), ("put", b)], w=[("zac", zb)])
                                    else:
                                        S.dve(lambda e, b=b, a=a, n=n, zb=zb: e.tensor_tensor(
                                            out=ztm[zb][:, 0:n], in0=sgm[a][:, 0:n], in1=put[b][:, 0:n], op=ALU.mult),
                                            r=[("sgm", a), ("put", b)], w=[("ztm", zb)])
                                        if a == 1:
                                            S.dve(lambda e, n=n, zb=zb: e.tensor_tensor(
                                                out=zac[zb][:, 0:n], in0=zac[zb][:, 0:n], in1=ztm[zb][:, 0:n], op=ALU.add),
                                                r=[("zac", zb), ("ztm", zb)], w=[("zac", zb)])
                                        else:
                                            S.dve(lambda e, n=n, zb=zb, fc=fc, s0=s0: e.tensor_tensor(
                                                out=zT[:, fc, s0:s0 + n], in0=zac[zb][:, 0:n], in1=ztm[zb][:, 0:n],
                                                op=ALU.add), r=[("zac", zb), ("ztm", zb)], w=[("zT", fc, s0)])
                        S.emit()
                    with contextlib.ExitStack() as ph:
                        S = Sched(nc)
                        wo2 = alloc(ph, "wo2", [128, KC, max(D - 512, 512)], BF16)
                        for k2 in range(512, D, 512):
                            S.dma("pool", lambda e, k2=k2: e.dma_start(
                                out=wo2[:, :, k2 - 512:k2],
                                in_=wout[l].rearrange("(k p) n -> p k n", p=128)[:, :, k2:k2 + 512]), w=[("wo", k2)])

                        def wo_ap(k, hf):
                            return Pw[:, k, :] if hf == 0 else wo2[:, k, (hf - 1) * 512:hf * 512]
                        gg = alloc(ph, "gg", [128, 2, D], F32)
                        gpo = alloc(ph, "gpo", [128, D], F32)
                        S.dma("act", lambda e: e.dma_start(out=gpo[:], in_=gvec[l, 1:2, :].partition_broadcast(128)),
                              w=["gpo"])
                        for s in range(2):
                            S.dma("act", lambda e, s=s: e.dma_start(
                                out=gg[:, s, :], in_=modd[l, s:s + 1, 2 * D:3 * D].partition_broadcast(128)), w=[("gg", s)])
                            S.dve(lambda e, s=s: e.tensor_tensor(out=gg[:, s, :], in0=gg[:, s, :], in1=gpo[:], op=ALU.mult),
                                  r=[("gg", s), "gpo"], w=[("gg", s)])
                        py = [palloc(ph, f"py{i}", [128, 1024], F32) for i in range(2)]

                        def src_out(ti, t, b):
                            tk = slice(t * 128, (t + 1) * 128)
                            for hf in range(D // 512):
                                for k in range(KC):
                                    S.pe(lambda e, k=k, hf=hf: e.matmul(
                                        py[b][:, hf * 512:(hf + 1) * 512], lhsT=zT[:, k, tk],
                                        rhs=wo_ap(k, hf), start=(k == 0), stop=(k == KC - 1)),
                                        r=[("wo", hf * 512)], w=[("py", b)])
                            return py[b][:, 0:D], [("py", b)]

                        def store_x(S_, t, xt, key):
                            S_.dma("sp", lambda e: e.dma_start(out=xs[t * 128:(t + 1) * 128, :], in_=xt[:]), r=[key])

                        norm_pipeline(S, ph, qtiles, src_out, gg, l, 1, store_x, True, "o", from_inputs=(l == 0))
                        S.emit()
            if stop_after == "mix":
                return nc
            with contextlib.ExitStack() as mlp:
                acc = alloc(mlp, "acc", [128, TT, D], F32)
                blocks = tok_blocks(c, with_ctx=not last)
                with contextlib.ExitStack() as ph:
                    S = Sched(nc)
                    w1s = [alloc(ph, f"w1s{i}", [128, KC, 512], BF16) for i in range(2)]
                    w2s = [alloc(ph, f"w2s{i}", [128, 4, D], BF16) for i in range(2)]
                    pu = [palloc(ph, f"pu{i}", [128, 512], F32) for i in range(3)]
                    pd = [palloc(ph, f"pd{i}", [128, 512], F32) for i in range(4)]
                    rr = [alloc(ph, f"rr{i}", [128, 512], F32) for i in range(2)]
                    aT = [alloc(ph, f"aT{i}", [128, 4, 512], BF16) for i in range(2)]
                    uc = 0
                    dc = 0
                    ac = 0
                    do_mod = not last
                    if do_mod:
                        wmbb = [alloc(ph, f"wmbb{i}", [128, KC, 512], BF16) for i in range(3)]
                        pmm = palloc(ph, "pmm", [128, 512], F32)
                        bsk = [alloc(ph, f"bsk{i}", [2, 512], F32) for i in range(2)]
                        mdk = [alloc(ph, f"mdk{i}", [2, 512], F32) for i in range(2)]
                        nblk_ = 6 * D // 512
                        per_fb = (nblk_ + c.FB - 1) // c.FB
                        mcnt = 0

                    def mod_block(j):
                        b3, b2 = j % 3, j % 2
                        S.dma("pool", lambda e: e.dma_start(
                            out=wmbb[b3][:], in_=wmod[l + 1].rearrange("(k p) n -> p k n", p=128)[:, :, j * 512:(j + 1) * 512]),
                            w=[("wmbb", b3)])
                        for a_ in range(2):
                            S.dma("sp", lambda e, a_=a_: e.dma_start(out=bsk[b2][a_:a_ + 1, :],
                                                                     in_=bmod[l + 1:l + 2, j * 512:(j + 1) * 512]),
                                  w=[("bsk", b2, a_)])
                        for k in range(KC):
                            S.pe(lambda e, k=k: e.matmul(pmm[0:2, :], lhsT=scTb[:, k, :], rhs=wmbb[b3][:, k, :],
                                                         start=(k == 0), stop=(k == KC - 1)), r=[("wmbb", b3)], w=["pmm"])
                        S.dve(lambda e: e.tensor_tensor(out=mdk[b2][:], in0=pmm[0:2, :], in1=bsk[b2][:], op=ALU.add),
                              r=["pmm", ("bsk", b2, 0), ("bsk", b2, 1)], w=[("mdk", b2)])
                        S.dma("sp", lambda e: e.dma_start(out=modd[l + 1, :, j * 512:(j + 1) * 512], in_=mdk[b2][:]),
                              r=[("mdk", b2)])

                    mitems = []
                    mstate = [0]

                    def fb_pre(fb):
                        wb = fb % 2
                        S.dma("pool", lambda e: e.dma_start(
                            out=w1s[wb][:], in_=wm1[l].rearrange("(k p) n -> p k n", p=128)[:, :, fb * 512:(fb + 1) * 512]),
                            w=[("w1s", wb)])
                        S.dma("pool", lambda e: e.dma_start(
                            out=w2s[wb][:], in_=wm2[l, fb * 512:(fb + 1) * 512, :].rearrange("(k p) n -> p k n", p=128)),
                            w=[("w2s", wb)])
                        if do_mod:
                            for _ in range(per_fb):
                                if mstate[0] < nblk_:
                                    mod_block(mstate[0])
                                    mstate[0] += 1

                    for fb in range(c.FB):
                        wb = fb % 2
                        for (s0, n, isc) in blocks:
                            mitems.append((fb, wb, s0, n))

                    def m_up(i):
                        fb, wb, s0, n = mitems[i]
                        ab = i % 2
                        for f4 in range(4):
                            b = (i * 4 + f4) % 3
                            rb = (i * 4 + f4) % 2
                            for k in range(KC):
                                S.pe(lambda e, b=b, k=k, f4=f4: e.matmul(
                                    pu[b][:, 0:n], lhsT=w1s[wb][:, k, f4 * 128:(f4 + 1) * 128], rhs=hT[:, k, s0:s0 + n],
                                    start=(k == 0), stop=(k == KC - 1)), r=[("w1s", wb)], w=[("pu", b)])
                            S.act(lambda e, b=b, rb=rb: e.activation(out=rr[rb][:, 0:n], in_=pu[b][:, 0:n], func=AF.Relu),
                                  r=[("pu", b)], w=[("rr", rb)])
                            S.dve(lambda e, rb=rb, f4=f4: e.tensor_tensor(
                                out=aT[ab][:, f4, 0:n], in0=rr[rb][:, 0:n], in1=rr[rb][:, 0:n], op=ALU.mult),
                                r=[("rr", rb)], w=[("aT", ab)])

                    def m_down(i):
                        fb, wb, s0, n = mitems[i]
                        ab = i % 2
                        for ti in range(n // 128):
                            t = s0 // 128 + ti
                            for hf in range(D // 512):
                                b = dcn[0] % 4
                                dcn[0] += 1
                                for f4 in range(4):
                                    S.pe(lambda e, b=b, f4=f4, ti=ti, hf=hf: e.matmul(
                                        pd[b][:, :], lhsT=aT[ab][:, f4, ti * 128:(ti + 1) * 128],
                                        rhs=w2s[wb][:, f4, hf * 512:(hf + 1) * 512], start=(f4 == 0), stop=(f4 == 3)),
                                        r=[("aT", ab), ("w2s", wb)], w=[("pd", b)])
                                if fb == 0:
                                    S.act(lambda e, b=b, t=t, hf=hf: e.activation(
                                        out=acc[:, t, hf * 512:(hf + 1) * 512], in_=pd[b][:, :], func=AF.Copy),
                                        r=[("pd", b)], w=[("acc", t, hf)])
                                else:
                                    S.dve(lambda e, b=b, t=t, hf=hf: e.tensor_tensor(
                                        out=acc[:, t, hf * 512:(hf + 1) * 512], in0=acc[:, t, hf * 512:(hf + 1) * 512],
                                        in1=pd[b][:, :], op=ALU.add), r=[("pd", b), ("acc", t, hf)], w=[("acc", t, hf)])

                    dcn = [0]
                    nm = len(mitems)
                    for step in range(nm + 1):
                        if step < nm:
                            fbs = mitems[step][0]
                            if step == 0 or mitems[step - 1][0] != fbs:
                                fb_pre(fbs)
                            m_up(step)
                        if step - 1 >= 0:
                            m_down(step - 1)
                    S.emit()
                with contextlib.ExitStack() as ph:
                    S = Sched(nc)
                    gg = alloc(ph, "gg2", [128, 2, D], F32)
                    gpo = alloc(ph, "gpo2", [128, D], F32)
                    S.dma("act", lambda e: e.dma_start(out=gpo[:], in_=gvec[l, 3:4, :].partition_broadcast(128)), w=["gpo"])
                    for s in range(2):
                        S.dma("act", lambda e, s=s: e.dma_start(
                            out=gg[:, s, :], in_=modd[l, s:s + 1, 5 * D:6 * D].partition_broadcast(128)), w=[("gg", s)])
                        S.dve(lambda e, s=s: e.tensor_tensor(out=gg[:, s, :], in0=gg[:, s, :], in1=gpo[:], op=ALU.mult),
                              r=[("gg", s), "gpo"], w=[("gg", s)])
                    def src_acc(ti, t, b):
                        return acc[:, t, :], []

                    def store_x2(S_, t, xt, key):
                        if last:
                            lo = (t - CT) * 128
                            S_.dma("sp", lambda e: e.dma_start(out=out[lo:lo + 128, :], in_=xt[:]), r=[key])
                        else:
                            S_.dma("sp", lambda e: e.dma_start(out=xs[t * 128:(t + 1) * 128, :], in_=xt[:]), r=[key])

                    if not last:
                        S.dma("pool", lambda e: e.dma_start(out=Pw[:], in_=w1v(l + 1, RQD, 512)), w=["Pw"])
                        pp_params(S, ph, l + 1)
                    norm_pipeline(S, ph, qtiles, src_acc, gg, l + 1, 0, store_x2, not last, "m")
                    S.emit()
    return nc


def host_consts(c):
    NL = c.NL
    ident = np.eye(128, dtype=np.float32)
    perm = np.zeros((128, 128), np.float32)
    for m in range(128):
        perm[(m % 64) ^ 16 | (m & 64), m] = 1.0
    bones = np.zeros((128, 128), np.float32)
    bones[:64, :64] = 1.0
    bones[64:, 64:] = 1.0
    t = np.arange(NL)
    pos = np.stack([t // 64, t % 64], -1).astype(np.float32)
    nf = 16
    inv = (np.float32(10000.0) ** (-np.arange(nf, dtype=np.float32) / nf)).astype(np.float32)
    cosd = np.zeros((128, NL), np.float32)
    sind = np.zeros((128, NL), np.float32)
    for p in range(128):
        i = p % 64
        a, s, f = i // 32, (i // 16) % 2, i % 16
        ang = (pos[:, a] * inv[f]).astype(np.float32)
        cosd[p] = np.cos(ang)
        sind[p] = np.sin(ang) * (-1.0 if s == 0 else 1.0)
    j = np.arange(128)[:, None].astype(np.float32)
    i = np.arange(128)[None, :].astype(np.float32)
    ctab = np.stack([np.maximum(i - j, 0), (i >= j).astype(np.float32),
                     np.maximum(j - i, 0), (j > i).astype(np.float32)]).astype(np.float32)
    posq = np.zeros((128, 128), np.float32)
    posq[:64] = np.arange(128)[None, :] + 1.0
    posq[64:] = 128.0 - np.arange(128)[None, :]
    posk = np.zeros((128, 8), np.float32)
    posk[:, 0:4] = (127.0 - np.arange(128))[:, None]
    posk[:, 4:8] = np.arange(128)[:, None]
    return dict(ident=ident, perm=perm, bones=bones, cosd=cosd, sind=sind, ctab=ctab, posq=posq, posk=posk)


def host_layout(c, inp):
    L = c.L
    w_in = np.asarray(inp["w_in"])
    sp = np.cumsum([0, 256, 256, 512, 512, 512, 512, 512, 512, 128, 128, c.D, c.D, c.D])
    cols = []
    rq0 = sp[0]
    for h in range(4):
        cols += list(range(rq0 + h * 64, rq0 + (h + 1) * 64)) * 2
    cols += list(range(sp[1], sp[2]))
    cols += list(range(sp[2], sp[3]))
    cols += list(range(sp[3], sp[4]))
    cols += list(range(sp[4], sp[5]))
    cols += list(range(sp[5], sp[6]))
    cols += list(range(sp[6], sp[7]))
    gq0 = sp[7]
    for cch in range(4):
        cols += list(range(gq0 + cch * 64, gq0 + (cch + 1) * 64))
        cols += list(range(gq0 + (4 + cch) * 64, gq0 + (5 + cch) * 64))
    cols += list(range(sp[8], sp[9]))
    cols += list(range(sp[9], sp[10]))
    assert len(cols) == GA
    w1 = np.zeros((L, c.D, W1C), np.float32)
    w1[:, :, :GA] = w_in[:, :, cols]
    w1[:, :, GA:GA + c.D] = w_in[:, :, sp[10]:sp[11]]
    w1[:, :, GB:GB + c.D] = w_in[:, :, sp[11]:sp[12]]
    w1[:, :, GC:GC + c.D] = w_in[:, :, sp[12]:sp[13]]
    wbr = np.stack([inp["w_br_ret"], inp["w_br_na"], inp["w_br_gqa"]], 1).astype(np.float32)
    gvec = np.stack([inp["g_pre_mix"], inp["g_post_mix"], inp["g_pre_mlp"], inp["g_post_mlp"]], 1).astype(np.float32)
    rdl = np.asarray(inp["ret_decay_logit"], np.float32).reshape(L, 8)
    rb = np.asarray(inp["na_rel_bias"], np.float32)
    col = np.arange(64)
    cs = np.clip(col - 8, 0, 64 - 16)
    inw = (col[None, :] >= cs[:, None]) & (col[None, :] < cs[:, None] + 16)
    dcol = np.clip(col[None, :] - col[:, None], -15, 15) + 15
    rbp = np.concatenate([rb, np.full((L, 8, 15, 1), NEG, np.float32)], -1)
    idx = np.where(inw, dcol, 31)
    g = rbp[:, :, :, idx]
    g = np.transpose(g, (0, 4, 1, 2, 3))
    nab = np.full((L, 64, 8, 17, 64), NEG, np.float32)
    nab[:, :, :, 1:16, :] = g
    nab = nab.reshape(L, 64, 8 * 17 * 64)
    gqn = np.stack([np.concatenate([inp["gqa_q_norm"]] * 2, -1), np.concatenate([inp["gqa_k_norm"]] * 2, -1)], 1)
    shared = dict(w1=w1, wbr=wbr, wout=np.asarray(inp["w_out"], np.float32),
                  wm1=np.asarray(inp["w_mlp_in"], np.float32), wm2=np.asarray(inp["w_mlp_out"], np.float32),
                  wmod=np.asarray(inp["w_mod"], np.float32), bmod=np.asarray(inp["b_mod"], np.float32),
                  gvec=gvec, rdl=rdl, nab=nab, gqn=gqn.astype(np.float32))
    shared.update(host_consts(c))
    return shared


_CACHE = {}


def kernel(**inp):
    c = Cfg()
    shared = host_layout(c, inp)
    x = np.asarray(inp["x"], np.float32)
    ctx = np.asarray(inp["ctx"], np.float32)
    cv = np.asarray(inp["c"], np.float32)
    cctx = np.asarray(inp["c_ctx"], np.float32)
    B = x.shape[0]
    in_maps = []
    for b in range(B):
        m = dict(shared)
        m["x_in"] = np.ascontiguousarray(x[b])
        m["ctx_in"] = np.ascontiguousarray(ctx[b])
        m["cc"] = np.stack([cv[b], cctx], 0)
        in_maps.append(m)
    nc = build(c)
    res = run_bass_kernel_spmd(nc, in_maps, core_ids=list(range(B)))
    return np.stack([r["out"] for r in res.results], 0).astype(np.float32)
```
